# Optimizing a Trainium2 kernel written in Bass

```python
import math
import jax, jax.numpy as jnp
from jax import lax
import numpy as np

D_MODEL = 1024
BATCH = 8
SEQ = 8192
DEPTH = 4

N_META = 16
BLOCK = 128
META_PAD = BLOCK - N_META
RET_HEADS = 4
RET_DK = D_MODEL // 8
RET_DV = D_MODEL // 4
DIFF_HEADS = 8
DIFF_HD = D_MODEL // 16
DIFF_VD = 2 * DIFF_HD
CONV_WIDTH = D_MODEL
CONV_K = 3
N_BRANCH = 3
BRANCH_W = D_MODEL
D_FF = 4 * D_MODEL
EPS = 1e-6
NEG_INF = -1e30
IN_WIDTHS = (RET_HEADS * RET_DK, RET_HEADS * RET_DK, RET_HEADS * RET_DV, RET_HEADS * RET_DV,
             2 * DIFF_HEADS * DIFF_HD, 2 * DIFF_HEADS * DIFF_HD, DIFF_HEADS * DIFF_VD,
             CONV_WIDTH, CONV_WIDTH, CONV_WIDTH, N_BRANCH * D_MODEL)
D_IN = sum(IN_WIDTHS)

kernel_name = "hybrid_retention_diffattn_shortconv_block"


def rms_norm(x, w):
    xf = x.astype(jnp.float32)
    y = xf * lax.rsqrt(jnp.mean(xf * xf, axis=-1, keepdims=True) + EPS)
    return (y * w.astype(jnp.float32)).astype(x.dtype)


def head_layer_norm(y):
    yf = y.astype(jnp.float32)
    mu = jnp.mean(yf, axis=-1, keepdims=True)
    var = jnp.mean(jnp.square(yf - mu), axis=-1, keepdims=True)
    return ((yf - mu) * lax.rsqrt(var + EPS)).astype(y.dtype)


def pad_front(t, n):
    return jnp.pad(t, [(0, 0), (n, 0)] + [(0, 0)] * (t.ndim - 2))


def column_bounds():
    bounds, start = [], 0
    for w in IN_WIDTHS:
        bounds.append((start, start + w))
        start += w
    return bounds


def retention(q, k, v):
    b = q.shape[0]
    dt = q.dtype
    k = k * (RET_DK ** -0.5)
    q, k, v = (pad_front(t, META_PAD) for t in (q, k, v))
    nc = q.shape[1] // BLOCK

    def chunks(t):
        return t.reshape(b, nc, BLOCK, RET_HEADS, t.shape[-1]).transpose(1, 0, 3, 2, 4)

    log_g = jnp.log1p(-jnp.exp2(-5.0 - jnp.arange(RET_HEADS, dtype=jnp.float32)))
    i = jnp.arange(BLOCK, dtype=jnp.float32)
    dist = i[:, None] - i[None, :]
    intra = jnp.where(dist >= 0, jnp.exp(log_g[:, None, None] * jnp.maximum(dist, 0.0)), 0.0).astype(dt)
    q_decay = jnp.exp(log_g[:, None] * (i + 1.0)).astype(dt)
    k_decay = jnp.exp(log_g[:, None] * (BLOCK - 1.0 - i)).astype(dt)
    s_decay = jnp.exp(log_g * BLOCK).astype(dt)

    def step(state, qkv):
        qc, kc, vc = qkv
        scores = jnp.einsum('bhid,bhjd->bhij', qc, kc) * intra
        out = (jnp.einsum('bhij,bhjv->bhiv', scores, vc)
               + jnp.einsum('bhid,bhdv->bhiv', qc * q_decay[:, :, None], state))
        state = state * s_decay[:, None, None] + jnp.einsum('bhjd,bhjv->bhdv', kc * k_decay[:, :, None], vc)
        return state, out

    s0 = jnp.zeros((b, RET_HEADS, RET_DK, RET_DV), dt)
    _, out = lax.scan(step, s0, (chunks(q), chunks(k), chunks(v)))
    out = out.transpose(1, 0, 3, 2, 4).reshape(b, nc * BLOCK, RET_HEADS, RET_DV)
    return out[:, META_PAD:]


def diff_attention(q, k, v, lam, subln_w, lam_init):
    b = q.shape[0]
    q, k, v = (pad_front(t, META_PAD) for t in (q, k, v))
    p = q.shape[1]
    nb = p // BLOCK
    kt = k.transpose(0, 2, 3, 1, 4)
    vt = v.transpose(0, 2, 1, 3)
    qb = q.reshape(b, nb, BLOCK, DIFF_HEADS, 2, DIFF_HD).transpose(1, 0, 3, 4, 2, 5)
    slopes = jnp.exp2(-8.0 / DIFF_HEADS * (jnp.arange(DIFF_HEADS, dtype=jnp.float32) + 1.0))
    kpos = jnp.arange(p)
    scale = DIFF_HD ** -0.5

    def block(args):
        qblk, bi = args
        qpos = bi * BLOCK + jnp.arange(BLOCK)
        dist = (qpos[:, None] - kpos[None, :]).astype(jnp.float32)
        valid = (dist >= 0) & (kpos[None, :] >= META_PAD)
        logits = (jnp.einsum('bhmqd,bhmkd->bhmqk', qblk, kt, preferred_element_type=jnp.float32) * scale
                  - slopes[:, None, None, None] * dist)
        probs = jax.nn.softmax(jnp.where(valid, logits, NEG_INF), axis=-1)
        attn = probs[:, :, 0] - lam * probs[:, :, 1]
        return jnp.einsum('bhqk,bhkv->bhqv', attn.astype(vt.dtype), vt)

    out = lax.map(block, (qb, jnp.arange(nb)))
    out = out.transpose(1, 0, 3, 2, 4).reshape(b, p, DIFF_HEADS, DIFF_VD)[:, META_PAD:]
    return rms_norm(out, subln_w) * (1.0 - lam_init)


def short_conv(u, w):
    return lax.conv_general_dilated(
        u, w[:, None, :].astype(u.dtype), window_strides=(1,), padding=[(CONV_K - 1, 0)],
        dimension_numbers=('NWC', 'WIO', 'NWC'), feature_group_count=u.shape[-1])


def hybrid_layer(x, layer, norm1_w, w_in, conv_w, lam_vecs, subln_w, w_branch, w_out, norm2_w, w_up, w_down):
    b, l, _ = x.shape
    h = rms_norm(x, norm1_w)
    (rq, rk, rv, rg, dq, dk, dv, cb, cc, cx, gates) = [h @ w_in[:, s:e] for s, e in column_bounds()]

    ret = retention(rq.reshape(b, l, RET_HEADS, RET_DK), rk.reshape(b, l, RET_HEADS, RET_DK),
                    rv.reshape(b, l, RET_HEADS, RET_DV))
    ret = head_layer_norm(ret).reshape(b, l, BRANCH_W) * jax.nn.silu(rg)

    lam_init = 0.8 - 0.6 * math.exp(-0.3 * layer)
    lv = lam_vecs.astype(jnp.float32)
    lam = jnp.exp(jnp.sum(lv[0] * lv[1])) - jnp.exp(jnp.sum(lv[2] * lv[3])) + lam_init
    diff = diff_attention(dq.reshape(b, l, DIFF_HEADS, 2, DIFF_HD), dk.reshape(b, l, DIFF_HEADS, 2, DIFF_HD),
                          dv.reshape(b, l, DIFF_HEADS, DIFF_VD), lam, subln_w, lam_init).reshape(b, l, BRANCH_W)

    conv = cb * short_conv(cc * cx, conv_w)

    g = jax.nn.sigmoid(gates.astype(jnp.float32)).astype(x.dtype).reshape(b, l, N_BRANCH, D_MODEL)
    merged = (g[:, :, 0] * (ret @ w_branch[0]) + g[:, :, 1] * (diff @ w_branch[1])
              + g[:, :, 2] * (conv @ w_branch[2]))
    x = x + merged @ w_out

    u = jnp.square(jax.nn.relu(rms_norm(x, norm2_w) @ w_up))
    return x + u @ w_down


def setup_inputs(seed: int = 0) -> dict:
    key = jax.random.key(seed)
    ks = jax.random.split(key, 13)
    nrm = jax.random.normal
    f32 = jnp.float32
    return {
        "x": nrm(ks[0], (BATCH, SEQ, D_MODEL), f32),
        "meta_tokens": nrm(ks[1], (N_META, D_MODEL), f32),
        "norm1_w": 1.0 + 0.02 * nrm(ks[2], (DEPTH, D_MODEL), f32),
        "w_in": nrm(ks[3], (DEPTH, D_MODEL, D_IN), f32) * D_MODEL ** -0.5,
        "conv_w": nrm(ks[4], (DEPTH, CONV_K, CONV_WIDTH), f32) * CONV_K ** -0.5,
        "diff_lambda": 0.1 * nrm(ks[5], (DEPTH, 4, DIFF_HD), f32),
        "diff_subln_w": 1.0 + 0.02 * nrm(ks[6], (DEPTH, DIFF_VD), f32),
        "w_branch": nrm(ks[7], (DEPTH, N_BRANCH, BRANCH_W, D_MODEL), f32) * BRANCH_W ** -0.5,
        "w_out": nrm(ks[8], (DEPTH, D_MODEL, D_MODEL), f32) * D_MODEL ** -0.5,
        "norm2_w": 1.0 + 0.02 * nrm(ks[9], (DEPTH, D_MODEL), f32),
        "w_up": nrm(ks[10], (DEPTH, D_MODEL, D_FF), f32) * D_MODEL ** -0.5,
        "w_down": nrm(ks[11], (DEPTH, D_FF, D_MODEL), f32) * D_FF ** -0.5,
        "final_norm_w": 1.0 + 0.02 * nrm(ks[12], (D_MODEL,), f32),
    }


def reference(x, meta_tokens, norm1_w, w_in, conv_w, diff_lambda, diff_subln_w, w_branch, w_out,
              norm2_w, w_up, w_down, final_norm_w):
    b = x.shape[0]
    meta = jnp.broadcast_to(meta_tokens.astype(x.dtype)[None], (b, N_META, D_MODEL))
    h = jnp.concatenate([meta, x], axis=1)
    for layer in range(DEPTH):
        h = hybrid_layer(h, layer, norm1_w[layer], w_in[layer], conv_w[layer], diff_lambda[layer],
                         diff_subln_w[layer], w_branch[layer], w_out[layer], norm2_w[layer],
                         w_up[layer], w_down[layer])
    return rms_norm(h[:, N_META:], final_norm_w)
```

```python
import math
from contextlib import ExitStack
import numpy as np
import concourse.bass as bass
import concourse.mybir as mybir
from concourse.bass_utils import run_bass_kernel_spmd

F32 = mybir.dt.float32
BF16 = mybir.dt.bfloat16
AF = mybir.ActivationFunctionType
ALU = mybir.AluOpType

D = 1024
KC = 8
NMETA = 16
PAD = 112
EPS = 1e-6
RH, RDK, RDV = 4, 128, 256
DH, DHD, DVD = 8, 64, 128
DFF = 4096
DIN = 12288
C_RQ, C_RK, C_RV, C_RG = 0, 512, 1024, 2048
C_DQ, C_DK, C_DV = 3072, 4096, 5120
C_CB, C_CC, C_CX = 6144, 7168, 8192
C_G = 9216


class Buf:
    __slots__ = ("name", "w", "r", "gen_waits")

    def __init__(self, name):
        self.name = name
        self.w = []
        self.r = []
        self.gen_waits = []


COMPUTE = ("pe", "act", "dve", "pool")


class Prog:
    NDMASEM = {"sp": 24, "pool": 24, "act": 16}
    ENGS = ("pe", "act", "dve", "pool", "sp")

    def __init__(self):
        self.segs = {}
        self.seg = None
        self.epoch = 0
        self.start_seg("main")

    def start_seg(self, name):
        self.seg = name
        self.epoch += 1
        self.segs[name] = {e: [] for e in self.ENGS}
        self.ops = self.segs[name]
        self.ndma = {e: 0 for e in self.NDMASEM}

    @staticmethod
    def _merge(lst, ev):
        if ev[0] == "c":
            for i, o in enumerate(lst):
                if o[0] == "c" and o[1] == ev[1] and o[3] == ev[3]:
                    if ev[2] > o[2]:
                        lst[i] = ev
                    return
        lst.append(ev)

    def op(self, eng, fn, reads=(), writes=(), newgen=(), dma=False):
        ops = self.ops[eng]
        idx = len(ops)
        ep = self.epoch
        waits = []
        for b in reads:
            waits.extend(b.w)
        for b in newgen:
            gw = [v for v in list(b.r) + list(b.w) if v[3] == ep]
            b.gen_waits = gw
            b.r = []
            b.w = []
            waits.extend(gw)
        for b in writes:
            waits.extend(b.gen_waits)
        if dma:
            n = self.ndma[eng]
            self.ndma[eng] += 1
            ns = self.NDMASEM[eng]
            if n >= ns:
                waits.append(("d", eng, n - ns, ep))
            ev = ("d", eng, n, ep)
        else:
            ev = ("c", eng, idx, ep)
        fw = []
        for wv in waits:
            if wv[3] != ep:
                continue
            if wv[0] == "c" and wv[1] == eng and eng == "pe":
                continue
            fw.append(wv)
        ops.append({"fn": fn, "waits": fw, "dma": (ev[2] if dma else None), "flag": False})
        for wv in fw:
            if wv[0] == "c":
                self.ops[wv[1]][wv[2]]["flag"] = True
        for b in reads:
            self._merge(b.r, ev)
        for b in list(newgen) + list(writes):
            self._merge(b.w, ev)
        return ev

    def wait_all(self, eng, events):
        ops = self.ops[eng]
        events = [wv for wv in events if wv[3] == self.epoch]
        for wv in events:
            if wv[0] == "c":
                self.ops[wv[1]][wv[2]]["flag"] = True
        ops.append({"fn": None, "waits": list(events), "dma": None, "flag": False})

    def pool_counts(self, segname):
        ns = self.NDMASEM["pool"]
        cnt = [0] * ns
        n = 0
        for o in self.segs[segname]["pool"]:
            if o["dma"] is not None:
                cnt[o["dma"] % ns] += 1
        return cnt

    def replay_seg(self, segname, ename, eng, csem, dsem, ll, pbase=None, pstride=None):
        segops = self.segs[segname]
        pref = {}
        for e in self.ENGS:
            c = 0
            arr = []
            for o in segops[e]:
                if o["flag"]:
                    c += 1
                arr.append(c)
            pref[e] = arr
        ns = self.NDMASEM
        waited = {}
        for o in segops[ename]:
            for wv in o["waits"]:
                if wv[0] == "c":
                    key = ("c", wv[1])
                    val = pref[wv[1]][wv[2]]
                    sem = csem[wv[1]]
                else:
                    n = wv[2]
                    slot = n % ns[wv[1]]
                    key = ("d", wv[1], slot)
                    val = 16 * (n // ns[wv[1]] + 1)
                    sem = dsem[wv[1]][slot]
                    if wv[1] == "pool" and pbase is not None:
                        if waited.get(key, 0) >= val:
                            continue
                        waited[key] = val
                        sv = val + 16 * pbase[slot]
                        if ll is not None and pstride[slot]:
                            sv = ll * (16 * pstride[slot]) + sv
                        eng.wait_ge(sem, sv)
                        continue
                if waited.get(key, 0) >= val:
                    continue
                waited[key] = val
                eng.wait_ge(sem, val)
            if o["fn"] is None:
                continue
            fn = o["fn"]
            ins = fn(eng, ll) if fn.__code__.co_argcount >= 2 and "ll" in fn.__code__.co_varnames[:2] else fn(eng)
            if o["dma"] is not None:
                n = o["dma"]
                ins.then_inc(dsem[ename][n % ns[ename]], 16)
            elif o["flag"]:
                ins.then_inc(csem[ename], 1)

    def replay(self, nc, block, csem, dsem, plan, barA, barB):
        decs = {"pe": block.tensor, "act": block.scalar, "dve": block.vector, "pool": block.gpsimd, "sp": block.sync}
        allsems = [csem[e] for e in self.ENGS] + [x for e in dsem if e != "pool" for x in dsem[e]]
        nsp = self.NDMASEM["pool"]
        bases = {}
        run = [0] * nsp
        for item in plan:
            cnt = self.pool_counts(item[1])
            if item[0] == "seg":
                bases[item[1]] = (list(run), [0] * nsp)
                run = [r + c for r, c in zip(run, cnt)]
            else:
                bases[item[1]] = (list(run), list(cnt))
                run = [r + c * item[2] for r, c in zip(run, cnt)]

        def sync_reset(ename, eng, count):
            eng.sem_inc(barA, 1)
            if ename == "sp":
                eng.wait_ge(barA, count * 5)
                for sm in allsems:
                    eng.sem_clear(sm)
                eng.sem_inc(barB, 1)
            else:
                eng.wait_ge(barB, count)

        def mk(ename):
            def body(eng):
                nreset = 0
                for i, item in enumerate(plan):
                    lastitem = (i == len(plan) - 1)
                    if item[0] == "seg":
                        self.replay_seg(item[1], ename, eng, csem, dsem, None, *bases[item[1]])
                        if not lastitem:
                            nreset += 1
                            sync_reset(ename, eng, nreset)
                    else:
                        cnt = item[2]
                        with eng.Fori(0, cnt) as l:
                            self.replay_seg(item[1], ename, eng, csem, dsem, l, *bases[item[1]])
                            sync_reset(ename, eng, l + (nreset + 1))
                        nreset += cnt
            return body

        for ename, dec in decs.items():
            dec(mk(ename))


def tile_cols(t):
    if t == 0:
        return 0, 128
    return 128 + 512 * (t - 1), 512


def retention_consts():
    log_g = np.log1p(-np.exp2(-5.0 - np.arange(RH, dtype=np.float64)))
    i = np.arange(128, dtype=np.float64)
    return log_g, i


def make_consts(NT):
    NB = 1 + 4 * NT
    log_g, i = retention_consts()
    cols = {}
    arrs = []
    off = 0

    def add(name, a):
        nonlocal off
        a = np.asarray(a, dtype=np.float32)
        assert a.shape[0] == 128
        cols[name] = (off, a.shape[1])
        arrs.append(a)
        off += a.shape[1]

    add("ident", np.eye(128))
    add("tri", (i[None, :] >= i[:, None]).astype(np.float32))
    dist = i[None, :] - i[:, None]
    intra = np.concatenate(
        [np.where(dist >= 0, np.exp(log_g[h] * np.maximum(dist, 0.0)), 0.0) for h in range(RH)], axis=1)
    add("intraT", intra)
    qd = np.concatenate([np.exp(log_g[h] * (i + 1.0)) for h in range(RH)])[None, :]
    add("qdec", np.repeat(qd, 128, axis=0))
    kd = np.stack([np.exp(log_g[h] * (127.0 - i)) for h in range(RH)], axis=1) * (RDK ** -0.5)
    add("kdec", np.repeat(kd, 128, axis=1))
    ND = NB + 4
    slopes = np.exp2(-(np.arange(DH, dtype=np.float64) + 1.0))
    tab = np.zeros((128, DH * ND))
    tab0 = np.zeros((128, DH * ND))
    for h in range(DH):
        for dd in range(ND):
            v = slopes[h] * (i - 128.0 * (dd - 3))
            tab[:, h * ND + dd] = v
            v0 = v.copy()
            v0[:PAD] = -30000.0
            tab0[:, h * ND + dd] = v0
    add("alibi", tab)
    add("alibi0", tab0)
    import ml_dtypes
    c = np.concatenate(arrs, axis=1).astype(np.float32)
    P = 128 + 512 * NT
    pos = np.arange(P)
    a = np.where(pos < 128, 0, ((pos - 128) % 512) // 128)
    qaug = np.stack([-slopes[h] * (128.0 * a + 64.0) for h in range(DH)], axis=0)
    aug = np.zeros((32, P), dtype=np.float32)
    for h in range(DH):
        for m in range(2):
            aug[h * 2 + m] = qaug[h]
            aug[16 + h * 2 + m] = 1.0
    return c, cols, aug.astype(ml_dtypes.bfloat16), ND


def layer_lam_init(l):
    return 0.8 - 0.6 * math.exp(-0.3 * l)


class Phases:
    AW = 52400

    def emit(self, eng, fn, reads=(), writes=(), newgen=(), dma=False):
        return self.pg.op(eng, fn, [r for r in reads], [w for w in writes], [g for g in newgen], dma)

    def barrier(self):
        pg = self.pg
        evs = []
        for e in COMPUTE:
            ops = pg.ops[e]
            for i in range(len(ops) - 1, -1, -1):
                if ops[i]["fn"] is not None and ops[i]["dma"] is None:
                    evs.append(("c", e, i, pg.epoch))
                    break
        for e, ns in pg.NDMASEM.items():
            n = pg.ndma[e]
            for k in range(max(0, n - ns), n):
                evs.append(("d", e, k, pg.epoch))
        for e in ("pe", "act", "dve", "pool", "sp"):
            pg.wait_all(e, evs)

    def alloc_global(self):
        nc = self.nc
        es = self.es
        self.arena = es.enter_context(nc.sbuf_tensor("arena", [128, self.AW], F32))
        self.aoff = 0
        self.ps = [es.enter_context(nc.psum_tensor(f"ps{i}", [128, 512], F32)) for i in range(8)]
        self.psb = [Buf(f"ps{i}") for i in range(8)]
        NCc = self.consts.shape[1]
        self.c32, self.c32b = self.t32("c32", NCc)
        self.par, self.parb = self.t32("par", self.NPAR)
        self.ones32, self.ones32b = self.t32("ones32", 128)
        self.ones16, self.ones16b = self.t16("ones16", 128)
        self.tri16, self.tri16b = self.t16("tri16", 128)
        self.lamt, self.lamtb = self.t32("lamt", 2 * self.DEPTH)
        self.lamtmp = self.t32("lamtmp", 128)
        self.lamsab = self.t32("lamsab", 4)
        self.WA, self.WAb = self.t16("WA", 32768)
        self.WB, self.WBb = self.t16("WB", 32768)
        self.aoff_phase = self.aoff

    def cc(self, name):
        o, n = self.ccols[name]
        return self.c32[:, o:o + n]

    def p_n1(self, l, kc):
        return self.par[:, l * 8 + kc: l * 8 + kc + 1]

    def p_n2(self, l, kc):
        o = self.PD * 8
        return self.par[:, o + l * 8 + kc: o + l * 8 + kc + 1]

    def p_fn(self, kc):
        o = self.PD * 16
        return self.par[:, o + kc: o + kc + 1]

    def p_conv(self, l, k, c):
        o = self.PD * 16 + 8 + l * 24 + k * 8 + c
        return self.par[:, o:o + 1]

    def p_subln(self, l):
        o = self.PD * 16 + 8 + self.PD * 24 + l
        return self.par[:, o:o + 1]

    def p_lamc(self, l, k):
        o = self.PD * 16 + 8 + self.PD * 24 + self.PD + self.PD * 256 + 2 * l + k
        return self.par[:, o:o + 1]

    def p_lv(self, l):
        o = self.PD * 16 + 8 + self.PD * 24 + self.PD + l * 256
        return self.par[:, o:o + 256]

    def phase_init(self):
        dr = self.dr
        c32, par = self.c32, self.par
        self.emit("sp", lambda e: e.dma_start(out=c32, in_=dr["consts"]), newgen=[self.c32b], dma=True)
        if self.mode == "loop":
            self.load_params(False)
        else:
            self.emit("sp", lambda e: e.dma_start(out=par, in_=dr["params"]), newgen=[self.parb], dma=True)
        o32, o16, tri16 = self.ones32, self.ones16, self.tri16
        self.emit("dve", lambda e: e.memset(o32, 1.0), newgen=[self.ones32b])
        self.emit("dve", lambda e: e.memset(o16, 1.0), newgen=[self.ones16b])
        tri = self.cc("tri")
        self.emit("dve", lambda e: e.tensor_copy(out=tri16, in_=tri), reads=[self.c32b], newgen=[self.tri16b])
        self.phase_reset()
        P = self.P
        augt, augb = self.t16("augt", P)
        self.emit("sp", lambda e: e.dma_start(out=augt[0:32, :], in_=dr["aug"]), newgen=[augb], dma=True)
        dq, dk = dr["dq"], dr["dk"]
        self.emit("sp", lambda e: e.dma_start(out=dq[:, 64, :], in_=augt[0:16, :]), reads=[augb],
                  newgen=[self.db_aug], dma=True)
        self.emit("sp", lambda e: e.dma_start(out=dk[:, 64, :], in_=augt[16:32, :]), reads=[augb],
                  writes=[self.db_aug], dma=True)
        if self.mode != "loop":
            for l in range(self.DEPTH):
                self.lam_compute(l)
        self.barrier()

    def lam_compute(self, l):
        tmp, tmpb = self.lamtmp
        sab, sabb = self.lamsab
        lv = self.p_lv(l)
        tA, tB = tmp[:, 0:64], tmp[:, 64:128]
        self.emit("dve", lambda e: e.tensor_tensor(out=tA, in0=lv[:, 0:64], in1=lv[:, 64:128], op=ALU.mult),
                  reads=[self.parb], newgen=[tmpb])
        self.emit("dve", lambda e: e.tensor_tensor(out=tB, in0=lv[:, 128:192], in1=lv[:, 192:256], op=ALU.mult),
                  reads=[self.parb], writes=[tmpb])
        self.emit("dve", lambda e: e.reduce_sum(out=sab[:, 0:1], in_=tA, axis=mybir.AxisListType.X),
                  reads=[tmpb], newgen=[sabb])
        self.emit("dve", lambda e: e.reduce_sum(out=sab[:, 1:2], in_=tB, axis=mybir.AxisListType.X),
                  reads=[tmpb], writes=[sabb])
        self.emit("act", lambda e: e.activation(out=sab[:, 2:4], in_=sab[:, 0:2], func=AF.Exp),
                  reads=[sabb], writes=[sabb])
        lt = self.lamt[:, 2 * l:2 * l + 1]
        self.emit("dve", lambda e: e.tensor_tensor(out=lt, in0=sab[:, 3:4], in1=sab[:, 2:3], op=ALU.subtract),
                  reads=[sabb], newgen=[self.lamtb])
        nli = self.p_lamc(l, 0)
        self.emit("dve", lambda e: e.tensor_scalar(out=lt, in0=lt, scalar1=nli, scalar2=None, op0=ALU.add),
                  reads=[self.lamtb, self.parb], writes=[self.lamtb])

    def load_params(self, dyn):
        dr, par = self.dr, self.par
        if dyn:
            self.emit("sp", lambda e, ll: e.dma_start(out=par, in_=dr["params"][ll, :, :]), newgen=[self.parb], dma=True)
        else:
            self.emit("sp", lambda e: e.dma_start(out=par, in_=dr["params"][0, :, :]), newgen=[self.parb], dma=True)

    def psum_alloc(self, banks):
        st = {"i": 0}

        def nxt():
            b = banks[st["i"] % len(banks)]
            st["i"] += 1
            return b
        return nxt

    def rmsnorm(self, xs, xsb, w, wcol, hs, hsb, sq_tiles, rstd, stat_bank):
        ps = self.ps[stat_bank]
        psb = self.psb[stat_bank]
        o32 = self.ones32
        for kc in range(KC):
            sq, sqb = sq_tiles[kc % len(sq_tiles)]
            self.emit("act", lambda e, sq=sq, kc=kc: e.activation(out=sq[:, 0:w], in_=xs[:, kc, 0:w], func=AF.Square),
                      reads=[xsb], newgen=[sqb])
            self.emit("pe", lambda e, sq=sq, kc=kc: e.matmul(ps[:, 0:w], o32, sq[:, 0:w], start=(kc == 0), stop=(kc == KC - 1)),
                      reads=[sqb, self.ones32b], **({"newgen": [psb]} if kc == 0 else {"writes": [psb]}))
        rs, rsb = rstd
        self.emit("act", lambda e: e.activation(out=rs[:, 0:w], in_=ps[:, 0:w], func=AF.Ln, scale=1.0 / D, bias=EPS),
                  reads=[psb], newgen=[rsb])
        self.emit("act", lambda e: e.activation(out=rs[:, 0:w], in_=rs[:, 0:w], func=AF.Exp, scale=-0.5),
                  reads=[rsb], writes=[rsb])
        for kc in range(KC):
            eng = "dve"
            self.emit(eng, lambda e, kc=kc: e.scalar_tensor_tensor(out=hs[:, kc, 0:w], in0=xs[:, kc, 0:w], scalar=wcol(kc),
                                                                     in1=rs[:, 0:w], op0=ALU.mult, op1=ALU.mult),
                      reads=[xsb, rsb, self.parb], **({"newgen": [hsb]} if kc == 0 else {"writes": [hsb]}))

    def load_w(self, dst3, dstb, name, l, sl, first):
        kcn = dst3.shape[1]
        loop = self.mode == "loop"
        src = self.dr[name + "_cur"] if loop else self.dr[name]
        lay = 0 if loop else l
        s2 = src[lay][sl] if not isinstance(sl, int) else src[lay, sl]
        rd = [self.wstb[name]] if loop else []
        n = dst3.shape[2]
        step = 8 if n <= 3072 else 4
        for i, k0 in enumerate(range(0, kcn, step)):
            k1 = min(kcn, k0 + step)
            srcv = s2[k0 * 128:k1 * 128, :].rearrange("(a p) n -> p a n", p=128)
            self.emit("pool", lambda e, k0=k0, k1=k1, srcv=srcv: e.dma_start(out=dst3[:, k0:k1, :], in_=srcv),
                      reads=rd, **({"newgen": [dstb]} if (first and i == 0) else {"writes": [dstb]}), dma=True)

    def stage_weights(self):
        dr = self.dr
        plan = [("w_in", [(slice(None), slice(c, c + 3072)) for c in range(0, DIN, 3072)]),
                ("w_branch", [(b,) for b in range(3)]),
                ("w_out", [(slice(None),)]),
                ("w_up", [(slice(None), slice(c, c + 1024)) for c in range(0, DFF, 1024)]),
                ("w_down", [(slice(r, r + 1024),) for r in range(0, DFF, 1024)])]
        for name, parts in plan:
            b = self.wstb[name]
            for i, sl in enumerate(parts):
                def fn(e, ll, name=name, sl=sl):
                    return e.dma_start(out=dr[name + "_cur"][(0,) + tuple(sl)], in_=dr[name][(ll,) + tuple(sl)])
                self.emit("act", fn, **({"newgen": [b]} if i == 0 else {"writes": [b]}), dma=True)

    def phase0(self):
        self.phase_reset()
        dr = self.dr
        ident = self.cc("ident")
        xin = [self.t32(f"xin{i}", 1024) for i in range(2)]
        xs_t = [self.t32(f"xs{i}", 8 * 256, a=8) for i in range(2)]
        hs_t = [self.t16(f"hs{i}", 8 * 256, a=8) for i in range(2)]
        sq_t = [self.t32(f"sq{i}", 256) for i in range(2)]
        rstd = self.t32("rstd", 256)
        nxt = self.psum_alloc([0, 1, 2, 3])
        ib = 0
        for it, (c0, w) in enumerate(self.subtiles(256)):
            xs, xsb = xs_t[it % 2]
            hs, hsb = hs_t[it % 2]
            for j in range(w // 128):
                xi, xib = xin[ib % 2]
                ib += 1
                blk = (c0 // 128) + j
                if blk == 0:
                    self.emit("dve", lambda e, xi=xi: e.memset(xi, 0.0), newgen=[xib])
                    self.emit("sp", lambda e, xi=xi: e.dma_start(out=xi[PAD:128, :], in_=dr["meta"]), writes=[xib], dma=True)
                else:
                    r0 = (blk - 1) * 128
                    self.emit("sp", lambda e, xi=xi, r0=r0: e.dma_start(out=xi, in_=dr["x"][r0:r0 + 128, :]),
                              newgen=[xib], dma=True)
                for half in range(2):
                    bk = nxt()
                    ps, psb = self.ps[bk], self.psb[bk]
                    for q in range(4):
                        kc = half * 4 + q
                        self.emit("pe", lambda e, ps=ps, q=q, kc=kc, xi=xi: e.transpose(
                            out=ps[:, q * 128:(q + 1) * 128], in_=xi[:, kc * 128:(kc + 1) * 128], identity=ident),
                            reads=[xib, self.c32b], **({"newgen": [psb]} if q == 0 else {"writes": [psb]}))
                    eng = "act" if half == 0 else "dve"
                    dst = xs[:, half * 4:half * 4 + 4, j * 128:(j + 1) * 128]
                    src = ps.rearrange("p (a b) -> p a b", a=4)
                    first = (j == 0 and half == 0)
                    if eng == "act":
                        fn = lambda e, dst=dst, src=src: e.activation(out=dst, in_=src, func=AF.Copy)
                    else:
                        fn = lambda e, dst=dst, src=src: e.tensor_copy(out=dst, in_=src)
                    self.emit(eng, fn, reads=[psb], **({"newgen": [xsb]} if first else {"writes": [xsb]}))
            self.store_x_and_norm(xs, xsb, hs, hsb, c0, w, lambda kc: self.p_n1(0, kc), "hT", sq_t, rstd, 4)
        self.barrier()

    def store_x_and_norm(self, xs, xsb, hs, hsb, c0, w, wcol, hname, sq_t, rstd, stat_bank, store_x=True):
        dr = self.dr
        if store_x:
            self.emit("sp", lambda e: e.dma_start(out=dr["xT"][:, :, c0:c0 + w], in_=xs[:, :, 0:w]),
                      reads=[xsb], writes=[self.tb("xT", c0)], dma=True)
        self.rmsnorm(xs, xsb, w, wcol, hs, hsb, sq_t, rstd, stat_bank)
        self.emit("sp", lambda e: e.dma_start(out=dr[hname][:, :, c0:c0 + w], in_=hs[:, :, 0:w]),
                  reads=[hsb], writes=[self.tb(hname, c0)], dma=True)
    def mm_group(self, ps_ap, psb, pairs, extra_reads=()):
        n = len(pairs)
        for i, (lt, rh, bufs) in enumerate(pairs):
            self.emit("pe", lambda e, lt=lt, rh=rh, i=i: e.matmul(ps_ap, lt, rh, start=(i == 0), stop=(i == n - 1)),
                      reads=list(bufs) + list(extra_reads), **({"newgen": [psb]} if i == 0 else {"writes": [psb]}))

    def phase_p1a(self, l):
        self.barrier()
        self.phase_reset()
        dr = self.dr
        TW = 256
        W3 = self.WA[:, 0:8 * 3072].rearrange("p (a b) -> p a b", a=8)
        Wb = self.WAb
        self.load_w(W3, Wb, "w_in", l, (slice(None), slice(0, 3072)), True)
        if True:
            W3n = self.WB[:, 0:8 * 3072].rearrange("p (a b) -> p a b", a=8)
            self.load_w(W3n, self.WBb, "w_in", l, (slice(None), slice(3072, 6144)), True)
        hT = [self.t16(f"hT{i}", 8 * TW, a=8) for i in range(2)]
        q_t = self.t16("q", 4 * TW, a=4)
        qd_t = self.t16("qd", 4 * TW, a=4)
        k_t = self.t16("kT", 4 * TW, a=4)
        ktm_t = self.t16("ktm", 2 * 512, a=2)
        vtm_t = self.t16("vtm", 2 * 1024, a=2)
        g_t = self.t32("g", 8 * TW, a=8)
        qf_t = [self.t32(f"qf{i}", TW) for i in range(2)]
        scm_t = [self.t16(f"scm{i}", 128) for i in range(4)]
        st32 = [self.t32(f"st32_{h}", 256) for h in range(4)]
        st16 = [self.t16(f"st16_{h}", 256) for h in range(4)]
        o_t = [self.t32(f"o{i}", 2 * TW, a=2) for i in range(2)]
        sqo_t = [self.t32(f"sqo{i}", 2 * TW, a=2) for i in range(2)]
        mean_t = self.t32("mean", TW)
        msq_t = self.t32("msq", TW)
        rs_t = self.t32("rs", TW)
        tt_t = [self.t32(f"tt{i}", TW) for i in range(2)]
        ret_t = [self.t16(f"ret{i}", 8 * TW, a=8) for i in range(2)]
        for h in range(4):
            s32, s32b = st32[h]
            s16, s16b = st16[h]
            self.emit("dve", lambda e, s32=s32: e.memset(s32, 0.0), newgen=[s32b])
            self.emit("dve", lambda e, s16=s16: e.memset(s16, 0.0), newgen=[s16b])
        log_g, _ = retention_consts()
        s_decay = [float(np.exp(log_g[h] * 128.0)) for h in range(4)]
        intraT, qdec, kdec = self.cc("intraT"), self.cc("qdec"), self.cc("kdec")
        nxt = self.psum_alloc([0, 1, 2])
        PS_S, PS_O, PS_U, PS_ST = 3, (4, 5), 6, 7
        tiles = self.subtiles(TW)
        for it, (c0, w) in enumerate(tiles):
            nb = w // 128
            ht, htb = hT[it % 2]
            self.emit("sp", lambda e, ht=ht, c0=c0, w=w: e.dma_start(out=ht[:, :, 0:w], in_=dr["hT"][:, :, c0:c0 + w]),
                      reads=[self.tb("hT", c0)], newgen=[htb], dma=True)
            import os
            ksub = int(os.environ.get("K_SUB", "9"))
            if ksub <= 0:
                continue
            q, qb = q_t
            qd, qdb = qd_t
            kT, kTb = k_t
            g, gb = g_t
            ktm, ktmb = ktm_t
            vtm, vtmb = vtm_t

            def fm(col, ht=ht, htb=htb, w=w):
                bk = nxt()
                ps, psb = self.ps[bk], self.psb[bk]
                self.mm_group(ps[:, 0:w], psb, [(W3[:, kc, col:col + 128], ht[:, kc, 0:w], [Wb, htb]) for kc in range(KC)])
                return ps, psb
            for h in range(4):
                ps, psb = fm(C_RQ + h * 128)
                qf, qfb = qf_t[h % 2]
                self.emit("act", lambda e, ps=ps, qf=qf, w=w: e.activation(out=qf[:, 0:w], in_=ps[:, 0:w], func=AF.Copy),
                          reads=[psb], newgen=[qfb])
                self.emit("pool", lambda e, qf=qf, h=h, w=w: e.tensor_copy(out=q[:, h, 0:w], in_=qf[:, 0:w]),
                          reads=[qfb], **({"newgen": [qb]} if h == 0 else {"writes": [qb]}))
                self.emit("dve", lambda e, qf=qf, h=h, w=w, nb=nb: e.tensor_tensor(
                    out=qd[:, h, 0:w].rearrange("p (a b) -> p a b", b=128),
                    in0=qf[:, 0:w].rearrange("p (a b) -> p a b", b=128),
                    in1=qdec[:, h * 128:(h + 1) * 128].unsqueeze(1).broadcast_to([128, nb, 128]), op=ALU.mult),
                    reads=[qfb, self.c32b], **({"newgen": [qdb]} if h == 0 else {"writes": [qdb]}))
            if ksub == 1 and os.environ.get("K_SUB2") == "a":
                continue
            for h in range(4):
                ps, psb = fm(C_RK + h * 128)
                self.emit("act", lambda e, ps=ps, h=h, w=w: e.activation(out=kT[:, h, 0:w], in_=ps[:, 0:w], func=AF.Copy,
                                                                         scale=RDK ** -0.5),
                          reads=[psb], **({"newgen": [kTb]} if h == 0 else {"writes": [kTb]}))
            for c in range(8):
                ps, psb = fm(C_RG + c * 128)
                self.emit("act", lambda e, ps=ps, c=c, w=w: e.activation(out=g[:, c, 0:w], in_=ps[:, 0:w], func=AF.Silu),
                          reads=[psb], **({"newgen": [gb]} if c == 0 else {"writes": [gb]}))
            if ksub == 1 and os.environ.get("K_SUB2") == "b":
                continue
            for j in range(nb):
                bk = nxt()
                ps, psb = self.ps[bk], self.psb[bk]
                self.mm_group(ps[:, 0:512], psb, [(ht[:, kc, j * 128:(j + 1) * 128], W3[:, kc, C_RK:C_RK + 512], [Wb, htb])
                                                  for kc in range(KC)])
                self.emit("dve", lambda e, ps=ps, j=j: e.tensor_tensor(out=ktm[:, j, :], in0=ps[:, 0:512], in1=kdec, op=ALU.mult),
                          reads=[psb, self.c32b], **({"newgen": [ktmb]} if j == 0 else {"writes": [ktmb]}))
                for half in range(2):
                    bk = nxt()
                    ps, psb = self.ps[bk], self.psb[bk]
                    self.mm_group(ps[:, 0:512], psb, [(ht[:, kc, j * 128:(j + 1) * 128],
                                                       W3[:, kc, C_RV + half * 512:C_RV + half * 512 + 512], [Wb, htb])
                                                      for kc in range(KC)])
                    self.emit("act", lambda e, ps=ps, j=j, half=half: e.activation(
                        out=vtm[:, j, half * 512:(half + 1) * 512], in_=ps[:, 0:512], func=AF.Copy),
                        reads=[psb], **({"newgen": [vtmb]} if (j == 0 and half == 0) else {"writes": [vtmb]}))
            rt, rtb = ret_t[it % 2]
            import os
            ksub = int(os.environ.get("K_SUB", "9"))
            if ksub <= 1:
                continue
            first_ret = True
            for hg in range(2):
                heads = (2 * hg, 2 * hg + 1)
                for j in range(nb):
                    cs = slice(j * 128, (j + 1) * 128)
                    pss, pssb = self.ps[PS_S], self.psb[PS_S]
                    for ih, h in enumerate(heads):
                        self.emit("pe", lambda e, h=h, ih=ih, cs=cs: e.matmul(pss[:, ih * 128:(ih + 1) * 128], kT[:, h, cs], q[:, h, cs],
                                                                             start=True, stop=True),
                                  reads=[kTb, qb], **({"newgen": [pssb]} if ih == 0 else {"writes": [pssb]}))
                    for ih, h in enumerate(heads):
                        sc, scb = scm_t[h]
                        self.emit("dve", lambda e, sc=sc, ih=ih, h=h: e.tensor_tensor(
                            out=sc, in0=pss[:, ih * 128:(ih + 1) * 128], in1=intraT[:, h * 128:(h + 1) * 128], op=ALU.mult),
                            reads=[pssb, self.c32b], newgen=[scb])
                    pu, pub = self.ps[PS_U], self.psb[PS_U]
                    for ih, h in enumerate(heads):
                        sc, scb = scm_t[h]
                        s16, s16b = st16[h]
                        po, pob = self.ps[PS_O[ih]], self.psb[PS_O[ih]]
                        for vh in range(2):
                            oc = slice(vh * TW + j * 128, vh * TW + (j + 1) * 128)
                            vcol = slice(h * 256 + vh * 128, h * 256 + (vh + 1) * 128)
                            ng = (j == 0 and vh == 0)
                            self.emit("pe", lambda e, po=po, oc=oc, vcol=vcol, j=j, sc=sc: e.matmul(
                                po[:, oc], vtm[:, j, vcol], sc, start=True, stop=False),
                                reads=[vtmb, scb], **({"newgen": [pob]} if ng else {"writes": [pob]}))
                            self.emit("pe", lambda e, po=po, oc=oc, vh=vh, h=h, cs=cs, s16=s16: e.matmul(
                                po[:, oc], s16[:, vh * 128:(vh + 1) * 128], qd[:, h, cs], start=False, stop=True),
                                reads=[s16b, qdb], writes=[pob])
                        self.emit("pe", lambda e, ih=ih, h=h, j=j: e.matmul(
                            pu[:, ih * 256:(ih + 1) * 256], ktm[:, j, h * 128:(h + 1) * 128], vtm[:, j, h * 256:(h + 1) * 256],
                            start=True, stop=True),
                            reads=[ktmb, vtmb], **({"newgen": [pub]} if ih == 0 else {"writes": [pub]}))
                    for ih, h in enumerate(heads):
                        s32, s32b = st32[h]
                        s16, s16b = st16[h]
                        self.emit("dve", lambda e, s32=s32, ih=ih, h=h: e.scalar_tensor_tensor(
                            out=s32, in0=s32, scalar=s_decay[h], in1=pu[:, ih * 256:(ih + 1) * 256], op0=ALU.mult, op1=ALU.add),
                            reads=[pub], newgen=[s32b])
                        self.emit("act", lambda e, s32=s32, s16=s16: e.activation(out=s16, in_=s32, func=AF.Copy),
                                  reads=[s32b], newgen=[s16b])
                for ih, h in enumerate(heads):
                    if ksub <= 2:
                        continue
                    po, pob = self.ps[PS_O[ih]], self.psb[PS_O[ih]]
                    o, ob = o_t[ih]
                    sqo, sqob = sqo_t[ih]
                    po3 = po[:, 0:2 * TW].rearrange("p (a b) -> p a b", a=2)
                    self.emit("act", lambda e, o=o, po3=po3, w=w: e.activation(out=o[:, :, 0:w], in_=po3[:, :, 0:w], func=AF.Copy),
                              reads=[pob], newgen=[ob])
                    self.emit("act", lambda e, sqo=sqo, po3=po3, w=w: e.activation(out=sqo[:, :, 0:w], in_=po3[:, :, 0:w], func=AF.Square),
                              reads=[pob], newgen=[sqob])
                    pst, pstb = self.ps[PS_ST], self.psb[PS_ST]
                    o32 = self.ones32
                    self.mm_group(pst[:, 0:w], pstb, [(o32, o[:, vh, 0:w], [ob, self.ones32b]) for vh in range(2)])
                    for vh in range(2):
                        self.emit("pe", lambda e, sqo=sqo, vh=vh, w=w: e.matmul(pst[:, TW:TW + w], o32, sqo[:, vh, 0:w],
                                                                              start=(vh == 0), stop=(vh == 1)),
                                  reads=[sqob, self.ones32b], writes=[pstb])
                    mean, meanb = mean_t
                    msq, msqb = msq_t
                    rs, rsb = rs_t
                    self.emit("act", lambda e, w=w: e.activation(out=mean[:, 0:w], in_=pst[:, 0:w], func=AF.Copy, scale=1.0 / RDV),
                              reads=[pstb], newgen=[meanb])
                    self.emit("dve", lambda e, w=w: e.tensor_tensor(out=msq[:, 0:w], in0=mean[:, 0:w], in1=mean[:, 0:w], op=ALU.mult),
                              reads=[meanb], newgen=[msqb])
                    self.emit("dve", lambda e, w=w: e.scalar_tensor_tensor(out=rs[:, 0:w], in0=pst[:, TW:TW + w], scalar=1.0 / RDV,
                                                                           in1=msq[:, 0:w], op0=ALU.mult, op1=ALU.subtract),
                              reads=[pstb, msqb], newgen=[rsb])
                    self.emit("act", lambda e, w=w: e.activation(out=rs[:, 0:w], in_=rs[:, 0:w], func=AF.Ln, scale=1.0, bias=EPS),
                              reads=[rsb], writes=[rsb])
                    self.emit("act", lambda e, w=w: e.activation(out=rs[:, 0:w], in_=rs[:, 0:w], func=AF.Exp, scale=-0.5),
                              reads=[rsb], writes=[rsb])
                    for vh in range(2):
                        tt, ttb = tt_t[vh]
                        c = h * 2 + vh
                        self.emit("dve", lambda e, tt=tt, o=o, vh=vh, w=w: e.tensor_tensor(out=tt[:, 0:w], in0=o[:, vh, 0:w],
                                                                                         in1=mean[:, 0:w], op=ALU.subtract),
                                  reads=[ob, meanb], newgen=[ttb])
                        self.emit("pool", lambda e, tt=tt, w=w: e.tensor_tensor(out=tt[:, 0:w], in0=tt[:, 0:w], in1=rs[:, 0:w], op=ALU.mult),
                                  reads=[ttb, rsb], writes=[ttb])
                        self.emit("pool", lambda e, tt=tt, c=c, w=w, rt=rt: e.tensor_tensor(out=rt[:, c, 0:w], in0=tt[:, 0:w], in1=g[:, c, 0:w],
                                                                                          op=ALU.mult),
                                  reads=[ttb, gb], **({"newgen": [rtb]} if first_ret else {"writes": [rtb]}))
                        first_ret = False
            if ksub <= 2:
                continue
            self.emit("sp", lambda e, rt=rt, c0=c0, w=w: e.dma_start(out=dr["retT"][:, :, c0:c0 + w], in_=rt[:, :, 0:w]),
                      reads=[rtb], writes=[self.tb("retT", c0)], dma=True)
    def phase_p1b(self, l):
        self.barrier()
        self.phase_reset()
        dr = self.dr
        TW = 512
        W3 = self.WB[:, 0:8 * 3072].rearrange("p (a b) -> p a b", a=8)
        Wb = self.WBb
        W3n = self.WA[:, 0:8 * 3072].rearrange("p (a b) -> p a b", a=8)
        self.load_w(W3n, self.WAb, "w_in", l, (slice(None), slice(6144, 9216)), True)
        hT = [self.t16(f"hT{i}", 8 * TW, a=8) for i in range(2)]
        stg = [self.t16(f"stg{i}", TW) for i in range(4)]
        vst = [self.t16(f"vst{i}", 1024) for i in range(2)]
        nxt = self.psum_alloc([0, 1, 2, 3, 4, 5])
        si = 0
        vi = 0
        for it, (c0, w) in enumerate(self.subtiles(TW)):
            nb = w // 128
            ht, htb = hT[it % 2]
            self.emit("sp", lambda e, ht=ht, c0=c0, w=w: e.dma_start(out=ht[:, :, 0:w], in_=dr["hT"][:, :, c0:c0 + w]),
                      reads=[self.tb("hT", c0)], newgen=[htb], dma=True)
            for which, cbase, scale in (("dq", 0, DHD ** -0.5), ("dk", 1024, 1.0)):
                for h in range(8):
                    bk = nxt()
                    ps, psb = self.ps[bk], self.psb[bk]
                    col = cbase + h * 128
                    self.mm_group(ps[:, 0:w], psb, [(W3[:, kc, col:col + 128], ht[:, kc, 0:w], [Wb, htb]) for kc in range(KC)])
                    st, stb = stg[si % 4]
                    si += 1
                    eng = "act" if h % 2 == 0 else "dve"
                    if eng == "act":
                        fn = lambda e, st=st, ps=ps, w=w, scale=scale: e.activation(out=st[:, 0:w], in_=ps[:, 0:w], func=AF.Copy, scale=scale)
                    else:
                        fn = lambda e, st=st, ps=ps, w=w, scale=scale: e.tensor_scalar(out=st[:, 0:w], in0=ps[:, 0:w], scalar1=scale,
                                                                                     scalar2=None, op0=ALU.mult)
                    self.emit(eng, fn, reads=[psb], newgen=[stb])
                    for m in range(2):
                        self.emit("sp", lambda e, st=st, which=which, h=h, m=m, c0=c0, w=w: e.dma_start(
                            out=dr[which][2 * h + m, 0:64, c0:c0 + w], in_=st[m * 64:(m + 1) * 64, 0:w]),
                            reads=[stb], writes=[self.tb(which, c0)], dma=True)
            for j in range(nb):
                vs, vsb = vst[vi % 2]
                vi += 1
                for half in range(2):
                    bk = nxt()
                    ps, psb = self.ps[bk], self.psb[bk]
                    cb = 2048 + half * 512
                    self.mm_group(ps[:, 0:512], psb, [(ht[:, kc, j * 128:(j + 1) * 128], W3[:, kc, cb:cb + 512], [Wb, htb])
                                                      for kc in range(KC)])
                    eng = "act" if half == 0 else "dve"
                    dst = vs[:, half * 512:(half + 1) * 512]
                    if eng == "act":
                        fn = lambda e, dst=dst, ps=ps: e.activation(out=dst, in_=ps[:, 0:512], func=AF.Copy)
                    else:
                        fn = lambda e, dst=dst, ps=ps: e.tensor_copy(out=dst, in_=ps[:, 0:512])
                    self.emit(eng, fn, reads=[psb], **({"newgen": [vsb]} if half == 0 else {"writes": [vsb]}))
                r0 = c0 + j * 128
                self.emit("sp", lambda e, vs=vs, r0=r0: e.dma_start(out=dr["dv"][r0:r0 + 128, :], in_=vs),
                          reads=[vsb], writes=[self.tb("dv", c0)], dma=True)

    def phase_p1c(self, l):
        self.barrier()
        self.phase_reset()
        dr = self.dr
        TW = 512
        W3 = self.WA[:, 0:8 * 3072].rearrange("p (a b) -> p a b", a=8)
        Wb = self.WAb
        hT = [self.t16(f"hT{i}", 8 * TW, a=8) for i in range(2)]
        cxs = [self.t32(f"cxs{i}", TW) for i in range(2)]
        us = [self.t32(f"u{i}", TW + 2) for i in range(2)]
        t1s = [self.t32(f"t1{i}", TW) for i in range(2)]
        halo, halob = self.t32("halo", 16, a=8)
        cv = [self.t16(f"cv{i}", 8 * TW, a=8) for i in range(2)]
        self.emit("dve", lambda e: e.memset(halo, 0.0), newgen=[halob])
        nxt = self.psum_alloc([0, 1, 2, 3, 4, 5])
        ci = 0
        for it, (c0, w) in enumerate(self.subtiles(TW)):
            ht, htb = hT[it % 2]
            self.emit("sp", lambda e, ht=ht, c0=c0, w=w: e.dma_start(out=ht[:, :, 0:w], in_=dr["hT"][:, :, c0:c0 + w]),
                      reads=[self.tb("hT", c0)], newgen=[htb], dma=True)
            co, cob = cv[it % 2]
            for c in range(8):
                pss = []
                for grp in range(3):
                    bk = nxt()
                    ps, psb = self.ps[bk], self.psb[bk]
                    col = grp * 1024 + c * 128
                    self.mm_group(ps[:, 0:w], psb, [(W3[:, kc, col:col + 128], ht[:, kc, 0:w], [Wb, htb]) for kc in range(KC)])
                    pss.append((ps, psb))
                (pb, pbb), (pc, pcb), (px, pxb) = pss
                cx, cxb = cxs[ci % 2]
                u, ub = us[ci % 2]
                t1, t1b = t1s[ci % 2]
                ci += 1
                self.emit("act", lambda e, cx=cx, px=px, w=w: e.activation(out=cx[:, 0:w], in_=px[:, 0:w], func=AF.Copy),
                          reads=[pxb], newgen=[cxb])
                self.emit("pool", lambda e, u=u, c=c: e.tensor_copy(out=u[:, 0:2], in_=halo[:, c, :]), reads=[halob], newgen=[ub])
                self.emit("dve", lambda e, u=u, pc=pc, cx=cx, w=w: e.tensor_tensor(out=u[:, 2:2 + w], in0=pc[:, 0:w], in1=cx[:, 0:w], op=ALU.mult),
                          reads=[pcb, cxb], writes=[ub])
                self.emit("pool", lambda e, u=u, c=c, w=w: e.tensor_copy(out=halo[:, c, :], in_=u[:, w:w + 2]), reads=[ub], writes=[halob])
                w0, w1, w2 = self.p_conv(l, 0, c), self.p_conv(l, 1, c), self.p_conv(l, 2, c)
                self.emit("pool", lambda e, t1=t1, u=u, w0=w0, w=w: e.tensor_scalar(out=t1[:, 0:w], in0=u[:, 0:w], scalar1=w0, scalar2=None,
                                                                                   op0=ALU.mult), reads=[ub, self.parb], newgen=[t1b])
                self.emit("dve", lambda e, t1=t1, u=u, w1=w1, w=w: e.scalar_tensor_tensor(out=t1[:, 0:w], in0=u[:, 1:1 + w], scalar=w1,
                                                                                          in1=t1[:, 0:w], op0=ALU.mult, op1=ALU.add),
                          reads=[ub, t1b, self.parb], writes=[t1b])
                self.emit("dve", lambda e, t1=t1, u=u, w2=w2, w=w: e.scalar_tensor_tensor(out=t1[:, 0:w], in0=u[:, 2:2 + w], scalar=w2,
                                                                                          in1=t1[:, 0:w], op0=ALU.mult, op1=ALU.add),
                          reads=[ub, t1b, self.parb], writes=[t1b])
                self.emit("dve", lambda e, co=co, c=c, pb=pb, t1=t1, w=w: e.tensor_tensor(out=co[:, c, 0:w], in0=pb[:, 0:w], in1=t1[:, 0:w],
                                                                                         op=ALU.mult),
                          reads=[pbb, t1b], **({"newgen": [cob]} if c == 0 else {"writes": [cob]}))
            self.emit("sp", lambda e, co=co, c0=c0, w=w: e.dma_start(out=dr["convT"][:, :, c0:c0 + w], in_=co[:, :, 0:w]),
                      reads=[cob], writes=[self.tb("convT", c0)], dma=True)

    def phase_att(self, l):
        self.barrier()
        self.phase_reset()
        dr = self.dr
        P, NB, ND = self.P, self.NB, self.ND
        TW = 512
        slots = []
        for s, (Wt, Wtb) in enumerate(((self.WA, self.WAb), (self.WB, self.WBb))):
            kA = [Wt[:, m * P:(m + 1) * P] for m in range(2)]
            V = Wt[:, 2 * P:3 * P].rearrange("p (a b) -> p a b", b=128)
            slots.append((kA, V, Wtb))
        assert 3 * P <= 32768
        qt = [[self.t16(f"q{s}{m}", TW) for m in range(2)] for s in range(2)]
        pts = [self.t16(f"pT{i}", TW) for i in range(4)]
        rc = [self.t32(f"rc{m}", TW) for m in range(2)]
        tm = [self.t32(f"tm{m}", TW) for m in range(2)]
        dd, ddb = self.t32("dd", TW)
        sq, sqb = self.t32("sq", TW)
        rs, rsb = self.t32("rs", TW)
        ost = [self.t16(f"ost{i}", TW) for i in range(2)]
        alibi, alibi0 = self.cc("alibi"), self.cc("alibi0")
        nxt = self.psum_alloc([0, 1, 2])
        PS_O, PS_SUM, PS_ST = (3, 4), (5, 6), 7
        neglam = self.lamt[:, 2 * l:2 * l + 1]
        subw = self.p_subln(l)
        post = 1.0 - layer_lam_init(l)
        tiles = self.subtiles(TW)
        pi = 0
        qi = 0
        oi = 0
        def load_kv(h):
            kA, V, Wtb = slots[h % 2]
            for m in range(2):
                self.emit("sp", lambda e, kA=kA, m=m, h=h: e.dma_start(out=kA[m][0:65, :], in_=dr["dk"][2 * h + m, :, :]),
                          reads=self.db["dk"] + [self.db_aug], **({"newgen": [Wtb]} if m == 0 else {"writes": [Wtb]}), dma=True)
            nsp = 8
            for part in range(nsp):
                b0 = (NB * part) // nsp
                b1 = (NB * (part + 1)) // nsp
                if b1 > b0:
                    src = dr["dv"][b0 * 128:b1 * 128, h * 128:(h + 1) * 128].rearrange("(a p) c -> p a c", p=128)
                    self.emit("sp", lambda e, V=V, b0=b0, b1=b1, src=src: e.dma_start(out=V[:, b0:b1, :], in_=src),
                              reads=self.db["dv"], writes=[Wtb], dma=True)
        load_kv(0)
        for h in range(DH):
            kA, V, Wtb = slots[h % 2]
            if h + 1 < DH:
                load_kv(h + 1)
            for T, (c0, w) in enumerate(tiles):
                qs = qt[qi % 2]
                qi += 1
                for m in range(2):
                    qa, qab = qs[m]
                    self.emit("sp", lambda e, qa=qa, m=m, h=h, c0=c0, w=w: e.dma_start(
                        out=qa[0:65, 0:w], in_=dr["dq"][2 * h + m, :, c0:c0 + w]),
                        reads=[self.tb("dq", c0), self.db_aug], newgen=[qab], dma=True)
                kb_first = c0 // 128
                kmax = kb_first + w // 128 - 1
                for kb in range(kmax + 1):
                    a_k = kb - kb_first
                    diag = a_k >= 0
                    col0 = 128 * a_k if diag else 0
                    ddi = (kb_first - kb) + 3
                    tab = alibi0 if kb == 0 else alibi
                    bias = tab[:, h * ND + ddi: h * ND + ddi + 1]
                    for m in range(2):
                        qa, qab = qs[m]
                        bk = nxt()
                        ps, psb = self.ps[bk], self.psb[bk]
                        self.emit("pe", lambda e, ps=ps, kA=kA, m=m, kb=kb, qa=qa, col0=col0, w=w: e.matmul(
                            ps[:, col0:w], kA[m][0:65, kb * 128:(kb + 1) * 128], qa[0:65, col0:w], start=True, stop=True),
                            reads=[Wtb, qab], newgen=[psb])
                        pt, ptb = pts[pi % 4]
                        pi += 1
                        self.emit("act", lambda e, pt=pt, ps=ps, col0=col0, w=w, bias=bias: e.activation(
                            out=pt[:, col0:w], in_=ps[:, col0:w], func=AF.Exp, bias=bias, scale=1.0),
                            reads=[psb, self.c32b], newgen=[ptb])
                        if diag:
                            self.emit("pool", lambda e, pt=pt, col0=col0: e.tensor_tensor(
                                out=pt[:, col0:col0 + 128], in0=pt[:, col0:col0 + 128], in1=self.tri16, op=ALU.mult),
                                reads=[ptb, self.tri16b], writes=[ptb])
                        po, pob = self.ps[PS_O[m]], self.psb[PS_O[m]]
                        psu, psub = self.ps[PS_SUM[m]], self.psb[PS_SUM[m]]
                        self.emit("pe", lambda e, po=po, V=V, kb=kb, pt=pt, col0=col0, w=w, kmax=kmax: e.matmul(
                            po[:, col0:w], V[:, kb, :], pt[:, col0:w], start=(kb == 0), stop=(kb == kmax)),
                            reads=[Wtb, ptb], **({"newgen": [pob]} if kb == 0 else {"writes": [pob]}))
                        self.emit("pe", lambda e, psu=psu, pt=pt, col0=col0, w=w, kb=kb, kmax=kmax: e.matmul(
                            psu[:, col0:w], self.ones16, pt[:, col0:w], start=(kb == 0), stop=(kb == kmax)),
                            reads=[self.ones16b, ptb], **({"newgen": [psub]} if kb == 0 else {"writes": [psub]}))
                for m in range(2):
                    r, rb = rc[m]
                    t, tb_ = tm[m]
                    psu, psub = self.ps[PS_SUM[m]], self.psb[PS_SUM[m]]
                    po, pob = self.ps[PS_O[m]], self.psb[PS_O[m]]
                    self.emit("dve", lambda e, r=r, psu=psu, w=w: e.tensor_scalar(out=r[:, 0:w], in0=psu[:, 0:w], scalar1=1e-30, scalar2=None,
                                                                                 op0=ALU.add), reads=[psub], newgen=[rb])
                    self.emit("dve", lambda e, r=r, w=w: e.reciprocal(out=r[:, 0:w], in_=r[:, 0:w]), reads=[rb], writes=[rb])
                    self.emit("dve", lambda e, t=t, po=po, r=r, w=w: e.tensor_tensor(out=t[:, 0:w], in0=po[:, 0:w], in1=r[:, 0:w], op=ALU.mult),
                              reads=[pob, rb], newgen=[tb_])
                self.emit("dve", lambda e, w=w: e.scalar_tensor_tensor(out=dd[:, 0:w], in0=tm[1][0][:, 0:w], scalar=neglam,
                                                                       in1=tm[0][0][:, 0:w], op0=ALU.mult, op1=ALU.add),
                          reads=[tm[0][1], tm[1][1], self.lamtb], newgen=[ddb])
                self.emit("act", lambda e, w=w: e.activation(out=sq[:, 0:w], in_=dd[:, 0:w], func=AF.Square), reads=[ddb], newgen=[sqb])
                pst, pstb = self.ps[PS_ST], self.psb[PS_ST]
                self.emit("pe", lambda e, w=w: e.matmul(pst[:, 0:w], self.ones32, sq[:, 0:w], start=True, stop=True),
                          reads=[self.ones32b, sqb], newgen=[pstb])
                self.emit("act", lambda e, w=w: e.activation(out=rs[:, 0:w], in_=pst[:, 0:w], func=AF.Ln, scale=1.0 / DVD, bias=EPS),
                          reads=[pstb], newgen=[rsb])
                self.emit("act", lambda e, w=w: e.activation(out=rs[:, 0:w], in_=rs[:, 0:w], func=AF.Exp, scale=-0.5,
                                                             bias=self.p_lamc(l, 1)), reads=[rsb, self.parb], writes=[rsb])
                os_, osb = ost[oi % 2]
                oi += 1
                self.emit("dve", lambda e, os_=os_, w=w: e.scalar_tensor_tensor(out=os_[:, 0:w], in0=dd[:, 0:w], scalar=subw, in1=rs[:, 0:w],
                                                                                 op0=ALU.mult, op1=ALU.mult),
                          reads=[ddb, rsb, self.parb], newgen=[osb])
                self.emit("sp", lambda e, os_=os_, h=h, c0=c0, w=w: e.dma_start(out=dr["diffT"][:, h, c0:c0 + w], in_=os_[:, 0:w]),
                          reads=[osb], writes=[self.tb("diffT", c0)], dma=True)
    def zero_pad_cols(self, xs, xsb):
        self.emit("dve", lambda e: e.memset(xs[:, :, 0:PAD], 0.0), reads=[xsb], writes=[xsb])

    def phase_p2(self, l):
        self.barrier()
        self.phase_reset()
        dr = self.dr
        TW = 256
        Wg = self.WB[:, 0:8 * 3072].rearrange("p (a b) -> p a b", a=8)
        Wo = self.WB[:, 8 * 3072:8 * 4096].rearrange("p (a b) -> p a b", a=8)
        Wbr = self.WA[:, 0:24 * 1024].rearrange("p (a b) -> p a b", a=24)
        self.load_w(Wg, self.WBb, "w_in", l, (slice(None), slice(C_G, C_G + 3072)), True)
        for b in range(3):
            self.load_w(Wbr[:, b * 8:(b + 1) * 8, :], self.WAb, "w_branch", l, b, b == 0)
        self.load_w(Wo, self.WBb, "w_out", l, (slice(None), slice(None)), False)
        hT = [self.t16(f"hT{i}", 8 * TW, a=8) for i in range(2)]
        brs = [self.t16(f"br{b}", 8 * TW, a=8) for b in range(3)]
        mg, mgb = self.t16("mg", 8 * TW, a=8)
        gs = [self.t32(f"g{b}", TW) for b in range(3)]
        mts = [self.t32(f"mt{i}", TW) for i in range(2)]
        xs, xsb = self.t32("xs", 8 * TW, a=8)
        h2 = [self.t16(f"h2{i}", 8 * TW, a=8) for i in range(2)]
        sq_t = [self.t32(f"sq{i}", TW) for i in range(2)]
        rstd = self.t32("rstd", TW)
        nxt = self.psum_alloc([0, 1, 2, 3, 4, 5, 6])
        names = ("retT", "diffT", "convT")
        for it, (c0, w) in enumerate(self.subtiles(TW)):
            ht, htb = hT[it % 2]
            self.emit("sp", lambda e, ht=ht, c0=c0, w=w: e.dma_start(out=ht[:, :, 0:w], in_=dr["hT"][:, :, c0:c0 + w]),
                      reads=[self.tb("hT", c0)], newgen=[htb], dma=True)
            for b in range(3):
                bt, btb = brs[b]
                self.emit("sp", lambda e, bt=bt, b=b, c0=c0, w=w: e.dma_start(out=bt[:, :, 0:w], in_=dr[names[b]][:, :, c0:c0 + w]),
                          reads=[self.tb(names[b], c0)], newgen=[btb], dma=True)
            self.emit("sp", lambda e, c0=c0, w=w: e.dma_start(out=xs[:, :, 0:w], in_=dr["xT"][:, :, c0:c0 + w]),
                      reads=[self.tb("xT", c0)], newgen=[xsb], dma=True)
            for n in range(8):
                pbs = []
                for b in range(3):
                    bk = nxt()
                    pg_, pgb = self.ps[bk], self.psb[bk]
                    col = b * 1024 + n * 128
                    self.mm_group(pg_[:, 0:w], pgb, [(Wg[:, kc, col:col + 128], ht[:, kc, 0:w], [self.WBb, htb]) for kc in range(KC)])
                    g, gb = gs[b]
                    self.emit("act", lambda e, g=g, pg_=pg_, w=w: e.activation(out=g[:, 0:w], in_=pg_[:, 0:w], func=AF.Sigmoid),
                              reads=[pgb], newgen=[gb])
                    bk = nxt()
                    pb, pbb = self.ps[bk], self.psb[bk]
                    bt, btb = brs[b]
                    self.mm_group(pb[:, 0:w], pbb, [(Wbr[:, b * 8 + kc, n * 128:(n + 1) * 128], bt[:, kc, 0:w], [self.WAb, btb])
                                                    for kc in range(KC)])
                    pbs.append((pb, pbb))
                m0, m0b = mts[0]
                m1, m1b = mts[1]
                self.emit("dve", lambda e, w=w, pb=pbs[0][0], g=gs[0][0]: e.tensor_tensor(out=m0[:, 0:w], in0=pb[:, 0:w], in1=g[:, 0:w], op=ALU.mult),
                          reads=[pbs[0][1], gs[0][1]], newgen=[m0b])
                self.emit("dve", lambda e, w=w, pb=pbs[1][0], g=gs[1][0]: e.tensor_tensor(out=m1[:, 0:w], in0=pb[:, 0:w], in1=g[:, 0:w], op=ALU.mult),
                          reads=[pbs[1][1], gs[1][1]], newgen=[m1b])
                self.emit("pool", lambda e, w=w: e.tensor_tensor(out=m0[:, 0:w], in0=m0[:, 0:w], in1=m1[:, 0:w], op=ALU.add),
                          reads=[m0b, m1b], writes=[m0b])
                self.emit("dve", lambda e, w=w, pb=pbs[2][0], g=gs[2][0]: e.tensor_tensor(out=m1[:, 0:w], in0=pb[:, 0:w], in1=g[:, 0:w], op=ALU.mult),
                          reads=[pbs[2][1], gs[2][1], m0b], newgen=[m1b])
                self.emit("pool", lambda e, w=w, n=n: e.tensor_tensor(out=mg[:, n, 0:w], in0=m0[:, 0:w], in1=m1[:, 0:w], op=ALU.add),
                          reads=[m0b, m1b], **({"newgen": [mgb]} if n == 0 else {"writes": [mgb]}))
            for n2 in range(8):
                bk = nxt()
                ps, psb = self.ps[bk], self.psb[bk]
                self.mm_group(ps[:, 0:w], psb, [(Wo[:, n, n2 * 128:(n2 + 1) * 128], mg[:, n, 0:w], [self.WBb, mgb]) for n in range(8)])
                self.emit("dve", lambda e, ps=ps, n2=n2, w=w: e.tensor_tensor(out=xs[:, n2, 0:w], in0=ps[:, 0:w], in1=xs[:, n2, 0:w], op=ALU.add),
                          reads=[psb, xsb], writes=[xsb])
            if c0 == 0:
                self.zero_pad_cols(xs, xsb)
            hs, hsb = h2[it % 2]
            self.store_x_and_norm(xs, xsb, hs, hsb, c0, w, lambda kc: self.p_n2(l, kc), "h2T", sq_t, rstd, 7)

    def phase_mlp(self, l):
        self.barrier()
        self.phase_reset()
        dr = self.dr
        TW = 256
        last = (l == self.DEPTH - 1) and self.mode == "fused"
        layer_mode = self.mode in ("layer", "loop")
        Wu = self.WA.rearrange("p (a b) -> p a b", a=8)
        Wd = self.WB.rearrange("p (a b) -> p a b", a=32)
        self.load_w(Wu, self.WAb, "w_up", l, (slice(None), slice(None)), True)
        self.load_w(Wd, self.WBb, "w_down", l, (slice(None), slice(None)), True)
        h2 = [self.t16(f"h2{i}", 8 * TW, a=8) for i in range(2)]
        xs, xsb = self.t32("xs", 8 * TW, a=8)
        u, ub = self.t16("u", 32 * TW, a=32)
        rts = [self.t32(f"r{i}", TW) for i in range(2)]
        sq_t = [self.t32(f"sq{i}", TW) for i in range(2)]
        rstd = self.t32("rstd", TW)
        if not last:
            hn = [self.t16(f"hn{i}", 8 * TW, a=8) for i in range(2)]
        else:
            ys, ysb = self.t32("ys", 8 * TW, a=8)
            ytm = [self.t32(f"ytm{i}", 1024) for i in range(2)]
            ident = self.cc("ident")
        nxt = self.psum_alloc([0, 1, 2, 3, 4, 5, 6])
        yi = 0
        for it, (c0, w) in enumerate(self.subtiles(TW)):
            ht, htb = h2[it % 2]
            self.emit("sp", lambda e, ht=ht, c0=c0, w=w: e.dma_start(out=ht[:, :, 0:w], in_=dr["h2T"][:, :, c0:c0 + w]),
                      reads=[self.tb("h2T", c0)], newgen=[htb], dma=True)
            self.emit("sp", lambda e, c0=c0, w=w: e.dma_start(out=xs[:, :, 0:w], in_=dr["xT"][:, :, c0:c0 + w]),
                      reads=[self.tb("xT", c0)], newgen=[xsb], dma=True)
            for c in range(32):
                bk = nxt()
                ps, psb = self.ps[bk], self.psb[bk]
                self.mm_group(ps[:, 0:w], psb, [(Wu[:, kc, c * 128:(c + 1) * 128], ht[:, kc, 0:w], [self.WAb, htb]) for kc in range(KC)])
                r, rb = rts[c % 2]
                if c % 2 == 0:
                    self.emit("act", lambda e, r=r, ps=ps, w=w: e.activation(out=r[:, 0:w], in_=ps[:, 0:w], func=AF.Relu),
                              reads=[psb], newgen=[rb])
                else:
                    self.emit("dve", lambda e, r=r, ps=ps, w=w: e.tensor_scalar(out=r[:, 0:w], in0=ps[:, 0:w], scalar1=0.0, scalar2=None,
                                                                               op0=ALU.max), reads=[psb], newgen=[rb])
                self.emit("pool", lambda e, r=r, c=c, w=w: e.tensor_tensor(out=u[:, c, 0:w], in0=r[:, 0:w], in1=r[:, 0:w], op=ALU.mult),
                          reads=[rb], **({"newgen": [ub]} if c == 0 else {"writes": [ub]}))
            for n2 in range(8):
                bk = nxt()
                ps, psb = self.ps[bk], self.psb[bk]
                self.mm_group(ps[:, 0:w], psb, [(Wd[:, c, n2 * 128:(n2 + 1) * 128], u[:, c, 0:w], [self.WBb, ub]) for c in range(32)])
                self.emit("dve", lambda e, ps=ps, n2=n2, w=w: e.tensor_tensor(out=xs[:, n2, 0:w], in0=ps[:, 0:w], in1=xs[:, n2, 0:w], op=ALU.add),
                          reads=[psb, xsb], writes=[xsb])
            if c0 == 0:
                self.zero_pad_cols(xs, xsb)
            if not last:
                hs, hsb = hn[it % 2]
                wc = (lambda kc: self.p_fn(kc)) if layer_mode else (lambda kc: self.p_n1(l + 1, kc))
                self.store_x_and_norm(xs, xsb, hs, hsb, c0, w, wc, "hT", sq_t, rstd, 7)
            elif c0 >= 128:
                self.rmsnorm(xs, xsb, w, lambda kc: self.p_fn(kc), ys, ysb, sq_t, rstd, 7)
                for j in range(w // 128):
                    yt, ytb = ytm[yi % 2]
                    yi += 1
                    for half in range(2):
                        bk = nxt()
                        ps, psb = self.ps[bk], self.psb[bk]
                        for q in range(4):
                            kc = half * 4 + q
                            self.emit("pe", lambda e, ps=ps, q=q, kc=kc, j=j: e.transpose(
                                out=ps[:, q * 128:(q + 1) * 128], in_=ys[:, kc, j * 128:(j + 1) * 128], identity=ident),
                                reads=[ysb, self.c32b], **({"newgen": [psb]} if q == 0 else {"writes": [psb]}))
                        dst = yt[:, half * 512:(half + 1) * 512]
                        if half == 0:
                            fn = lambda e, dst=dst, ps=ps: e.activation(out=dst, in_=ps[:, 0:512], func=AF.Copy)
                            eng = "act"
                        else:
                            fn = lambda e, dst=dst, ps=ps: e.tensor_copy(out=dst, in_=ps[:, 0:512])
                            eng = "dve"
                        self.emit(eng, fn, reads=[psb], **({"newgen": [ytb]} if half == 0 else {"writes": [ytb]}))
                    r0 = c0 - 128 + j * 128
                    self.emit("sp", lambda e, yt=yt, r0=r0: e.dma_start(out=dr["out"][r0:r0 + 128, :], in_=yt),
                              reads=[ytb], writes=[self.tb("out", c0)], dma=True)

    def phase_copy_in(self, with_h):
        self.barrier()
        self.phase_reset()
        dr = self.dr
        TW = 256
        xs_t = [self.t32(f"cx{i}", 8 * TW, a=8) for i in range(2)]
        hs_t = [self.t16(f"ch{i}", 8 * TW, a=8) for i in range(2)]
        sq_t = [self.t32(f"sq{i}", TW) for i in range(2)]
        rstd = self.t32("rstd", TW)
        for it, (c0, w) in enumerate(self.subtiles(TW)):
            xs, xsb = xs_t[it % 2]
            self.emit("sp", lambda e, xs=xs, c0=c0, w=w: e.dma_start(out=xs[:, :, 0:w], in_=dr["xT_in"][:, :, c0:c0 + w]),
                      newgen=[xsb], dma=True)
            if with_h:
                hs, hsb = hs_t[it % 2]
                self.store_x_and_norm(xs, xsb, hs, hsb, c0, w, lambda kc: self.p_n1(0, kc), "hT", sq_t, rstd, 7)
            else:
                self.emit("sp", lambda e, xs=xs, c0=c0, w=w: e.dma_start(out=dr["xT"][:, :, c0:c0 + w], in_=xs[:, :, 0:w]),
                          reads=[xsb], writes=[self.tb("xT", c0)], dma=True)

    def phase_post(self):
        self.barrier()
        self.phase_reset()
        dr = self.dr
        TW = 256
        xs_t = [self.t32(f"px{i}", 8 * TW, a=8) for i in range(2)]
        ys, ysb = self.t32("ys", 8 * TW, a=8)
        ytm = [self.t32(f"ytm{i}", 1024) for i in range(2)]
        sq_t = [self.t32(f"sq{i}", TW) for i in range(2)]
        rstd = self.t32("rstd", TW)
        nxt = self.psum_alloc([0, 1, 2, 3, 4, 5, 6])
        yi = 0
        for it, (c0, w) in enumerate(self.subtiles(TW)):
            if c0 < 128:
                continue
            xs, xsb = xs_t[it % 2]
            self.emit("sp", lambda e, xs=xs, c0=c0, w=w: e.dma_start(out=xs[:, :, 0:w], in_=dr["xT"][:, :, c0:c0 + w]),
                      reads=[self.tb("xT", c0)], newgen=[xsb], dma=True)
            yi = self.final_out(xs, xsb, ys, ysb, ytm, sq_t, rstd, nxt, c0, w, yi)

    def final_out(self, xs, xsb, ys, ysb, ytm, sq_t, rstd, nxt, c0, w, yi):
        dr = self.dr
        ident = self.cc("ident")
        self.rmsnorm(xs, xsb, w, lambda kc: self.p_fn(kc), ys, ysb, sq_t, rstd, 7)
        for j in range(w // 128):
            yt, ytb = ytm[yi % 2]
            yi += 1
            for half in range(2):
                bk = nxt()
                ps, psb = self.ps[bk], self.psb[bk]
                for q in range(4):
                    kc = half * 4 + q
                    self.emit("pe", lambda e, ps=ps, q=q, kc=kc, j=j: e.transpose(
                        out=ps[:, q * 128:(q + 1) * 128], in_=ys[:, kc, j * 128:(j + 1) * 128], identity=ident),
                        reads=[ysb, self.c32b], **({"newgen": [psb]} if q == 0 else {"writes": [psb]}))
                dst = yt[:, half * 512:(half + 1) * 512]
                if half == 0:
                    fn = lambda e, dst=dst, ps=ps: e.activation(out=dst, in_=ps[:, 0:512], func=AF.Copy)
                    eng = "act"
                else:
                    fn = lambda e, dst=dst, ps=ps: e.tensor_copy(out=dst, in_=ps[:, 0:512])
                    eng = "dve"
                self.emit(eng, fn, reads=[psb], **({"newgen": [ytb]} if half == 0 else {"writes": [ytb]}))
            r0 = c0 - 128 + j * 128
            self.emit("sp", lambda e, yt=yt, r0=r0: e.dma_start(out=dr["out"][r0:r0 + 128, :], in_=yt),
                      reads=[ytb], writes=[self.tb("out", c0)], dma=True)
        return yi

    def finish(self):
        self.barrier()


class Builder(Phases):
    def __init__(self, NT, DEPTH, debug=False, mode="fused"):
        self.mode = mode
        self.NT = NT
        self.DEPTH = DEPTH
        self.PD = 1 if mode == "loop" else DEPTH
        self.P = 128 + 512 * NT
        self.NB = 1 + 4 * NT
        self.debug = debug
        self.consts, self.ccols, self.aug, self.ND = make_consts(NT)

    def carve(self, words):
        off = self.aoff
        self.aoff += words
        assert self.aoff <= self.AW, (self.aoff, self.AW)
        return self.arena[:, off:off + words]

    def t32(self, name, n, a=None):
        v = self.carve(n)
        if a is not None:
            v = v.rearrange("p (a b) -> p a b", a=a)
        return (v, Buf(name))

    def t16(self, name, n, a=None):
        assert n % 2 == 0
        v = self.carve(n // 2).bitcast(BF16)
        if a is not None:
            v = v.rearrange("p (a b) -> p a b", a=a)
        return (v, Buf(name))

    def phase_reset(self):
        self.aoff = self.aoff_phase

    def tb(self, name, c0):
        t = 0 if c0 < 128 else 1 + (c0 - 128) // 512
        return self.db[name][t]

    def subtiles(self, tw):
        res = [(0, 128)]
        c = 128
        while c < self.P:
            res.append((c, tw))
            c += tw
        return res

    def build(self):
        NT, DEPTH, P = self.NT, self.DEPTH, self.P
        nc = bass.Bass("TRN2", target_bir_lowering=False)
        self.nc = nc
        S = P - 128
        dbg = "ExternalOutput" if self.debug else "Internal"
        mode = self.mode
        dr = {}
        if mode in ("fused", "pre", "loop"):
            dr["x"] = nc.dram_tensor("x", [S, D], F32, kind="ExternalInput").ap()
            dr["meta"] = nc.dram_tensor("meta", [NMETA, D], F32, kind="ExternalInput").ap()
        if mode in ("layer", "post"):
            dr["xT_in"] = nc.dram_tensor("xT_in", [128, KC, P], F32, kind="ExternalInput").ap()
        if mode in ("fused", "layer", "loop"):
            dr["w_in"] = nc.dram_tensor("w_in", [DEPTH, D, DIN], F32, kind="ExternalInput").ap()
            dr["w_branch"] = nc.dram_tensor("w_branch", [DEPTH, 3, D, D], F32, kind="ExternalInput").ap()
            dr["w_out"] = nc.dram_tensor("w_out", [DEPTH, D, D], F32, kind="ExternalInput").ap()
            dr["w_up"] = nc.dram_tensor("w_up", [DEPTH, D, DFF], F32, kind="ExternalInput").ap()
            dr["w_down"] = nc.dram_tensor("w_down", [DEPTH, DFF, D], F32, kind="ExternalInput").ap()
        PD = self.PD
        self.NPAR = PD * 8 * 2 + 8 + PD * 24 + PD + PD * 256 + 2 * PD
        if mode == "loop":
            dr["w_in_cur"] = nc.dram_tensor("w_in_cur", [1, D, DIN], F32, kind="Internal").ap()
            dr["w_branch_cur"] = nc.dram_tensor("w_branch_cur", [1, 3, D, D], F32, kind="Internal").ap()
            dr["w_out_cur"] = nc.dram_tensor("w_out_cur", [1, D, D], F32, kind="Internal").ap()
            dr["w_up_cur"] = nc.dram_tensor("w_up_cur", [1, D, DFF], F32, kind="Internal").ap()
            dr["w_down_cur"] = nc.dram_tensor("w_down_cur", [1, DFF, D], F32, kind="Internal").ap()
            self.wstb = {k: Buf("wst_" + k) for k in ("w_in", "w_branch", "w_out", "w_up", "w_down")}
            dr["params"] = nc.dram_tensor("params", [DEPTH, 128, self.NPAR], F32, kind="ExternalInput").ap()
        else:
            dr["params"] = nc.dram_tensor("params", [128, self.NPAR], F32, kind="ExternalInput").ap()
        dr["consts"] = nc.dram_tensor("consts", list(self.consts.shape), F32, kind="ExternalInput").ap()
        dr["aug"] = nc.dram_tensor("aug", [32, P], BF16, kind="ExternalInput").ap()
        if mode in ("fused", "post", "loop"):
            dr["out"] = nc.dram_tensor("out", [S, D], F32, kind="ExternalOutput").ap()
        xk = "ExternalOutput" if mode in ("pre", "layer") else dbg
        dr["xT"] = nc.dram_tensor("xT", [128, KC, P], F32, kind=xk).ap()
        dr["hT"] = nc.dram_tensor("hT", [128, KC, P], BF16, kind=dbg).ap()
        dr["h2T"] = nc.dram_tensor("h2T", [128, KC, P], BF16, kind=dbg).ap()
        dr["retT"] = nc.dram_tensor("retT", [128, KC, P], BF16, kind=dbg).ap()
        dr["convT"] = nc.dram_tensor("convT", [128, KC, P], BF16, kind=dbg).ap()
        dr["diffT"] = nc.dram_tensor("diffT", [128, KC, P], BF16, kind=dbg).ap()
        dr["dq"] = nc.dram_tensor("dq", [DH * 2, 65, P], BF16, kind=dbg).ap()
        dr["dk"] = nc.dram_tensor("dk", [DH * 2, 65, P], BF16, kind=dbg).ap()
        dr["dv"] = nc.dram_tensor("dv", [P, D], BF16, kind=dbg).ap()
        self.dr = dr
        self.db = {k: [Buf(f"{k}{t}") for t in range(NT + 1)] for k in
                   ("xT", "hT", "h2T", "retT", "convT", "diffT", "dq", "dk", "dv", "out")}
        self.db_aug = Buf("augrows")

        self.pg = Prog()
        with ExitStack() as es:
            self.es = es
            self.alloc_global()
            plan = [("seg", "main")]
            if mode == "loop":
                self.pg.start_seg("pre")
                self.phase_init()
                self.phase0()
                self.finish()
                import os
                if int(os.environ.get("K_LT", "9")) == -3:
                    self.pg.start_seg("pre2")
                    self.finish()
                self.pg.start_seg("body")
                lt = int(os.environ.get("K_LT", "9"))
                self.load_params(True)
                if lt >= 1:
                    self.stage_weights()
                self.lam_compute(0)
                for i, ph in enumerate((self.phase_p1a, self.phase_p1b, self.phase_p1c, self.phase_att, self.phase_p2, self.phase_mlp)):
                    if lt >= 2 + i:
                        ph(0)
                self.finish()
                self.pg.start_seg("post")
                self.phase_post()
                self.finish()
                plan = [("seg", "pre"), ("loop", "body", DEPTH), ("seg", "post")]
                if lt == -1:
                    plan = [("seg", "pre"), ("seg", "post")]
                if lt == -2:
                    plan = [("seg", "pre")]
                if lt == -3:
                    plan = [("seg", "pre"), ("seg", "pre2")]
            else:
                self.phase_init()
                if mode in ("fused", "pre"):
                    self.phase0()
                if mode == "layer":
                    self.phase_copy_in(True)
                if mode == "post":
                    self.phase_copy_in(False)
                if mode in ("fused", "layer"):
                    for l in range(DEPTH):
                        for ph in (self.phase_p1a, self.phase_p1b, self.phase_p1c, self.phase_att, self.phase_p2, self.phase_mlp):
                            ph(l)
                if mode == "post":
                    self.phase_post()
                self.finish()
            csem = {e: es.enter_context(nc.semaphore(f"c_{e}")) for e in Prog.ENGS}
            dsem = {e: [es.enter_context(nc.semaphore(f"d_{e}{i}")) for i in range(n)]
                    for e, n in Prog.NDMASEM.items()}
            barA = es.enter_context(nc.semaphore("barA"))
            barB = es.enter_context(nc.semaphore("barB"))
            with nc.Block() as block:
                self.pg.replay(nc, block, csem, dsem, plan, barA, barB)
        return nc


def make_params(DEPTH, norm1_w, conv_w, diff_lambda, diff_subln_w, norm2_w, final_norm_w, layer0=0):
    cols = []
    cols.append(norm1_w[:DEPTH].reshape(DEPTH, 8, 128).transpose(2, 0, 1).reshape(128, DEPTH * 8))
    cols.append(norm2_w[:DEPTH].reshape(DEPTH, 8, 128).transpose(2, 0, 1).reshape(128, DEPTH * 8))
    cols.append(final_norm_w.reshape(8, 128).T)
    cols.append(conv_w[:DEPTH].reshape(DEPTH, 3, 8, 128).transpose(3, 0, 1, 2).reshape(128, DEPTH * 24))
    cols.append(diff_subln_w[:DEPTH].T)
    cols.append(np.broadcast_to(diff_lambda[:DEPTH].reshape(1, DEPTH * 256), (128, DEPTH * 256)))
    lamc = np.zeros((128, 2 * DEPTH), np.float32)
    for l in range(DEPTH):
        li = layer_lam_init(layer0 + l)
        lamc[:, 2 * l] = -li
        lamc[:, 2 * l + 1] = math.log(1.0 - li)
    cols.append(lamc)
    return np.ascontiguousarray(np.concatenate(cols, axis=1), dtype=np.float32)


_CACHE = {}


def run(inputs, NT, DEPTH, ncores, debug=False, trace=False):
    key = (NT, DEPTH, debug)
    if key not in _CACHE:
        b = Builder(NT, DEPTH, debug)
        nc = b.build()
        _CACHE[key] = (b, nc)
    b, nc = _CACHE[key]
    f = lambda a: np.ascontiguousarray(np.asarray(a, dtype=np.float32))
    S = 512 * NT
    params = make_params(DEPTH, f(inputs["norm1_w"]), f(inputs["conv_w"]), f(inputs["diff_lambda"]),
                         f(inputs["diff_subln_w"]), f(inputs["norm2_w"]), f(inputs["final_norm_w"]))
    shared = {
        "meta": f(inputs["meta_tokens"]),
        "w_in": f(inputs["w_in"])[:DEPTH], "w_branch": f(inputs["w_branch"])[:DEPTH],
        "w_out": f(inputs["w_out"])[:DEPTH], "w_up": f(inputs["w_up"])[:DEPTH], "w_down": f(inputs["w_down"])[:DEPTH],
        "params": params, "consts": b.consts, "aug": b.aug,
    }
    x = f(inputs["x"])
    in_maps = [dict(shared, x=np.ascontiguousarray(x[c, :S])) for c in range(ncores)]
    res = run_bass_kernel_spmd(nc, in_maps, core_ids=list(range(ncores)), **({"trace": True} if trace else {}))
    out = np.stack([res.results[c]["out"] for c in range(ncores)], axis=0)
    return out, res


def _prog(NT, mode):
    key = (NT, mode)
    if key not in _CACHE:
        b = Builder(NT, 1, False, mode=mode)
        _CACHE[key] = (b, b.build())
    return _CACHE[key]


def run_unfused(inputs, NT, DEPTH, ncores):
    f = lambda a: np.ascontiguousarray(np.asarray(a, dtype=np.float32))
    S = 512 * NT
    x = f(inputs["x"])
    n1, n2, fnw = f(inputs["norm1_w"]), f(inputs["norm2_w"]), f(inputs["final_norm_w"])
    cw, dl, sl = f(inputs["conv_w"]), f(inputs["diff_lambda"]), f(inputs["diff_subln_w"])
    cores = list(range(ncores))

    def params(l, nxt):
        return make_params(1, n1[l:l + 1], cw[l:l + 1], dl[l:l + 1], sl[l:l + 1], n2[l:l + 1], nxt, layer0=l)

    b, nc = _prog(NT, "pre")
    shared = {"meta": f(inputs["meta_tokens"]), "params": params(0, fnw), "consts": b.consts, "aug": b.aug}
    res = run_bass_kernel_spmd(nc, [dict(shared, x=np.ascontiguousarray(x[c, :S])) for c in cores], core_ids=cores)
    xT = [res.results[c]["xT"] for c in cores]
    b, nc = _prog(NT, "layer")
    for l in range(DEPTH):
        nxt = n1[l + 1] if l + 1 < DEPTH else fnw
        shared = {"w_in": f(inputs["w_in"][l:l + 1]), "w_branch": f(inputs["w_branch"][l:l + 1]),
                  "w_out": f(inputs["w_out"][l:l + 1]), "w_up": f(inputs["w_up"][l:l + 1]),
                  "w_down": f(inputs["w_down"][l:l + 1]), "params": params(l, nxt), "consts": b.consts, "aug": b.aug}
        res = run_bass_kernel_spmd(nc, [dict(shared, xT_in=xT[c]) for c in cores], core_ids=cores)
        xT = [res.results[c]["xT"] for c in cores]
    b, nc = _prog(NT, "post")
    shared = {"params": params(0, fnw), "consts": b.consts, "aug": b.aug}
    res = run_bass_kernel_spmd(nc, [dict(shared, xT_in=xT[c]) for c in cores], core_ids=cores)
    return np.stack([res.results[c]["out"] for c in cores], axis=0)


def run_loop(inputs, NT, DEPTH, ncores, trace=False):
    key = (NT, DEPTH, "loop")
    if key not in _CACHE:
        b = Builder(NT, DEPTH, False, mode="loop")
        _CACHE[key] = (b, b.build())
    b, nc = _CACHE[key]
    f = lambda a: np.ascontiguousarray(np.asarray(a, dtype=np.float32))
    S = 512 * NT
    x = f(inputs["x"])
    n1, n2, fnw = f(inputs["norm1_w"]), f(inputs["norm2_w"]), f(inputs["final_norm_w"])
    cw, dl, sl = f(inputs["conv_w"]), f(inputs["diff_lambda"]), f(inputs["diff_subln_w"])
    pars = []
    for l in range(DEPTH):
        nxt = n1[l + 1] if l + 1 < DEPTH else fnw
        pars.append(make_params(1, n1[l:l + 1], cw[l:l + 1], dl[l:l + 1], sl[l:l + 1], n2[l:l + 1], nxt, layer0=l))
    shared = {
        "meta": f(inputs["meta_tokens"]),
        "w_in": f(inputs["w_in"])[:DEPTH], "w_branch": f(inputs["w_branch"])[:DEPTH],
        "w_out": f(inputs["w_out"])[:DEPTH], "w_up": f(inputs["w_up"])[:DEPTH], "w_down": f(inputs["w_down"])[:DEPTH],
        "params": np.ascontiguousarray(np.stack(pars, axis=0)), "consts": b.consts, "aug": b.aug,
    }
    cores = list(range(ncores))
    in_maps = [dict(shared, x=np.ascontiguousarray(x[c, :S])) for c in cores]
    res = run_bass_kernel_spmd(nc, in_maps, core_ids=cores, **({"trace": True} if trace else {}))
    return np.stack([res.results[c]["out"] for c in cores], axis=0), res


def kernel(x, meta_tokens, norm1_w, w_in, conv_w, diff_lambda, diff_subln_w, w_branch, w_out,
           norm2_w, w_up, w_down, final_norm_w):
    inputs = dict(x=x, meta_tokens=meta_tokens, norm1_w=norm1_w, w_in=w_in, conv_w=conv_w, diff_lambda=diff_lambda,
                  diff_subln_w=diff_subln_w, w_branch=w_branch, w_out=w_out, norm2_w=norm2_w, w_up=w_up,
                  w_down=w_down, final_norm_w=final_norm_w)
    out, _ = run_loop(inputs, NT=16, DEPTH=4, ncores=8)
    return out.astype(np.float32)
```

```python
import math
from contextlib import ExitStack
import numpy as np
import concourse.bass as bass
import concourse.mybir as mybir
from concourse.bass_utils import run_bass_kernel_spmd

F32 = mybir.dt.float32
BF16 = mybir.dt.bfloat16
AF = mybir.ActivationFunctionType
ALU = mybir.AluOpType

D = 1024
KC = 8
NMETA = 16
PAD = 112
EPS = 1e-6
RH, RDK, RDV = 4, 128, 256
DH, DHD, DVD = 8, 64, 128
DFF = 4096
DIN = 12288
C_RQ, C_RK, C_RV, C_RG = 0, 512, 1024, 2048
C_DQ, C_DK, C_DV = 3072, 4096, 5120
C_CB, C_CC, C_CX = 6144, 7168, 8192
C_G = 9216


class Buf:
    __slots__ = ("name", "w", "r", "gen_waits")

    def __init__(self, name):
        self.name = name
        self.w = []
        self.r = []
        self.gen_waits = []


COMPUTE = ("pe", "act", "dve", "pool")


class Prog:
    NDMASEM = {"sp": 24, "pool": 24, "act": 16}
    ENGS = ("pe", "act", "dve", "pool", "sp")

    def __init__(self):
        self.segs = {}
        self.seg = None
        self.epoch = 0
        self.start_seg("main")

    def start_seg(self, name):
        self.seg = name
        self.epoch += 1
        self.segs[name] = {e: [] for e in self.ENGS}
        self.ops = self.segs[name]
        self.ndma = {e: 0 for e in self.NDMASEM}

    @staticmethod
    def _merge(lst, ev):
        if ev[0] == "c":
            for i, o in enumerate(lst):
                if o[0] == "c" and o[1] == ev[1] and o[3] == ev[3]:
                    if ev[2] > o[2]:
                        lst[i] = ev
                    return
        lst.append(ev)

    def op(self, eng, fn, reads=(), writes=(), newgen=(), dma=False):
        ops = self.ops[eng]
        idx = len(ops)
        ep = self.epoch
        waits = []
        for b in reads:
            waits.extend(b.w)
        for b in newgen:
            gw = [v for v in list(b.r) + list(b.w) if v[3] == ep]
            b.gen_waits = gw
            b.r = []
            b.w = []
            waits.extend(gw)
        for b in writes:
            waits.extend(b.gen_waits)
        if dma:
            n = self.ndma[eng]
            self.ndma[eng] += 1
            ns = self.NDMASEM[eng]
            if n >= ns:
                waits.append(("d", eng, n - ns, ep))
            ev = ("d", eng, n, ep)
        else:
            ev = ("c", eng, idx, ep)
        fw = []
        for wv in waits:
            if wv[3] != ep:
                continue
            if wv[0] == "c" and wv[1] == eng and eng == "pe":
                continue
            fw.append(wv)
        ops.append({"fn": fn, "waits": fw, "dma": (ev[2] if dma else None), "flag": False})
        for wv in fw:
            if wv[0] == "c":
                self.ops[wv[1]][wv[2]]["flag"] = True
        for b in reads:
            self._merge(b.r, ev)
        for b in list(newgen) + list(writes):
            self._merge(b.w, ev)
        return ev

    def wait_all(self, eng, events):
        ops = self.ops[eng]
        events = [wv for wv in events if wv[3] == self.epoch]
        for wv in events:
            if wv[0] == "c":
                self.ops[wv[1]][wv[2]]["flag"] = True
        ops.append({"fn": None, "waits": list(events), "dma": None, "flag": False})

    def pool_counts(self, segname):
        ns = self.NDMASEM["pool"]
        cnt = [0] * ns
        n = 0
        for o in self.segs[segname]["pool"]:
            if o["dma"] is not None:
                cnt[o["dma"] % ns] += 1
        return cnt

    def replay_seg(self, segname, ename, eng, csem, dsem, ll, pbase=None, pstride=None):
        segops = self.segs[segname]
        pref = {}
        for e in self.ENGS:
            c = 0
            arr = []
            for o in segops[e]:
                if o["flag"]:
                    c += 1
                arr.append(c)
            pref[e] = arr
        ns = self.NDMASEM
        waited = {}
        for o in segops[ename]:
            for wv in o["waits"]:
                if wv[0] == "c":
                    key = ("c", wv[1])
                    val = pref[wv[1]][wv[2]]
                    sem = csem[wv[1]]
                else:
                    n = wv[2]
                    slot = n % ns[wv[1]]
                    key = ("d", wv[1], slot)
                    val = 16 * (n // ns[wv[1]] + 1)
                    sem = dsem[wv[1]][slot]
                    if wv[1] == "pool" and pbase is not None:
                        if waited.get(key, 0) >= val:
                            continue
                        waited[key] = val
                        sv = val + 16 * pbase[slot]
                        if ll is not None and pstride[slot]:
                            sv = ll * (16 * pstride[slot]) + sv
                        eng.wait_ge(sem, sv)
                        continue
                if waited.get(key, 0) >= val:
                    continue
                waited[key] = val
                eng.wait_ge(sem, val)
            if o["fn"] is None:
                continue
            fn = o["fn"]
            ins = fn(eng, ll) if fn.__code__.co_argcount >= 2 and "ll" in fn.__code__.co_varnames[:2] else fn(eng)
            if o["dma"] is not None:
                n = o["dma"]
                ins.then_inc(dsem[ename][n % ns[ename]], 16)
            elif o["flag"]:
                ins.then_inc(csem[ename], 1)

    def replay(self, nc, block, csem, dsem, plan, barA, barB):
        decs = {"pe": block.tensor, "act": block.scalar, "dve": block.vector, "pool": block.gpsimd, "sp": block.sync}
        allsems = [csem[e] for e in self.ENGS] + [x for e in dsem if e != "pool" for x in dsem[e]]
        nsp = self.NDMASEM["pool"]
        bases = {}
        run = [0] * nsp
        for item in plan:
            cnt = self.pool_counts(item[1])
            if item[0] == "seg":
                bases[item[1]] = (list(run), [0] * nsp)
                run = [r + c for r, c in zip(run, cnt)]
            else:
                bases[item[1]] = (list(run), list(cnt))
                run = [r + c * item[2] for r, c in zip(run, cnt)]

        def sync_reset(ename, eng, count):
            eng.sem_inc(barA, 1)
            if ename == "sp":
                eng.wait_ge(barA, count * 5)
                for sm in allsems:
                    eng.sem_clear(sm)
                eng.sem_inc(barB, 1)
            else:
                eng.wait_ge(barB, count)

        def mk(ename):
            def body(eng):
                nreset = 0
                for i, item in enumerate(plan):
                    lastitem = (i == len(plan) - 1)
                    if item[0] == "seg":
                        self.replay_seg(item[1], ename, eng, csem, dsem, None, *bases[item[1]])
                        if not lastitem:
                            nreset += 1
                            sync_reset(ename, eng, nreset)
                    else:
                        cnt = item[2]
                        with eng.Fori(0, cnt) as l:
                            self.replay_seg(item[1], ename, eng, csem, dsem, l, *bases[item[1]])
                            sync_reset(ename, eng, l + (nreset + 1))
                        nreset += cnt
            return body

        for ename, dec in decs.items():
            dec(mk(ename))


def tile_cols(t):
    if t == 0:
        return 0, 128
    return 128 + 512 * (t - 1), 512


def retention_consts():
    log_g = np.log1p(-np.exp2(-5.0 - np.arange(RH, dtype=np.float64)))
    i = np.arange(128, dtype=np.float64)
    return log_g, i


def make_consts(NT):
    NB = 1 + 4 * NT
    log_g, i = retention_consts()
    cols = {}
    arrs = []
    off = 0

    def add(name, a):
        nonlocal off
        a = np.asarray(a, dtype=np.float32)
        assert a.shape[0] == 128
        cols[name] = (off, a.shape[1])
        arrs.append(a)
        off += a.shape[1]

    add("ident", np.eye(128))
    add("tri", (i[None, :] >= i[:, None]).astype(np.float32))
    dist = i[None, :] - i[:, None]
    intra = np.concatenate(
        [np.where(dist >= 0, np.exp(log_g[h] * np.maximum(dist, 0.0)), 0.0) for h in range(RH)], axis=1)
    add("intraT", intra)
    qd = np.concatenate([np.exp(log_g[h] * (i + 1.0)) for h in range(RH)])[None, :]
    add("qdec", np.repeat(qd, 128, axis=0))
    kd = np.stack([np.exp(log_g[h] * (127.0 - i)) for h in range(RH)], axis=1) * (RDK ** -0.5)
    add("kdec", np.repeat(kd, 128, axis=1))
    ND = NB + 4
    slopes = np.exp2(-(np.arange(DH, dtype=np.float64) + 1.0))
    tab = np.zeros((128, DH * ND))
    tab0 = np.zeros((128, DH * ND))
    for h in range(DH):
        for dd in range(ND):
            v = slopes[h] * (i - 128.0 * (dd - 3))
            tab[:, h * ND + dd] = v
            v0 = v.copy()
            v0[:PAD] = -30000.0
            tab0[:, h * ND + dd] = v0
    add("alibi", tab)
    add("alibi0", tab0)
    import ml_dtypes
    c = np.concatenate(arrs, axis=1).astype(np.float32)
    P = 128 + 512 * NT
    pos = np.arange(P)
    a = np.where(pos < 128, 0, ((pos - 128) % 512) // 128)
    qaug = np.stack([-slopes[h] * (128.0 * a + 64.0) for h in range(DH)], axis=0)
    aug = np.zeros((32, P), dtype=np.float32)
    for h in range(DH):
        for m in range(2):
            aug[h * 2 + m] = qaug[h]
            aug[16 + h * 2 + m] = 1.0
    return c, cols, aug.astype(ml_dtypes.bfloat16), ND


def layer_lam_init(l):
    return 0.8 - 0.6 * math.exp(-0.3 * l)


class Phases:
    AW = 52400

    def emit(self, eng, fn, reads=(), writes=(), newgen=(), dma=False):
        return self.pg.op(eng, fn, [r for r in reads], [w for w in writes], [g for g in newgen], dma)

    def barrier(self):
        pg = self.pg
        evs = []
        for e in COMPUTE:
            ops = pg.ops[e]
            for i in range(len(ops) - 1, -1, -1):
                if ops[i]["fn"] is not None and ops[i]["dma"] is None:
                    evs.append(("c", e, i, pg.epoch))
                    break
        for e, ns in pg.NDMASEM.items():
            n = pg.ndma[e]
            for k in range(max(0, n - ns), n):
                evs.append(("d", e, k, pg.epoch))
        for e in ("pe", "act", "dve", "pool", "sp"):
            pg.wait_all(e, evs)

    def alloc_global(self):
        nc = self.nc
        es = self.es
        self.arena = es.enter_context(nc.sbuf_tensor("arena", [128, self.AW], F32))
        self.aoff = 0
        self.ps = [es.enter_context(nc.psum_tensor(f"ps{i}", [128, 512], F32)) for i in range(8)]
        self.psb = [Buf(f"ps{i}") for i in range(8)]
        NCc = self.consts.shape[1]
        self.c32, self.c32b = self.t32("c32", NCc)
        self.par, self.parb = self.t32("par", self.NPAR)
        self.ones32, self.ones32b = self.t32("ones32", 128)
        self.ones16, self.ones16b = self.t16("ones16", 128)
        self.tri16, self.tri16b = self.t16("tri16", 128)
        self.lamt, self.lamtb = self.t32("lamt", 2 * self.DEPTH)
        self.lamtmp = self.t32("lamtmp", 128)
        self.lamsab = self.t32("lamsab", 4)
        self.WA, self.WAb = self.t16("WA", 32768)
        self.WB, self.WBb = self.t16("WB", 32768)
        self.aoff_phase = self.aoff

    def cc(self, name):
        o, n = self.ccols[name]
        return self.c32[:, o:o + n]

    def p_n1(self, l, kc):
        return self.par[:, l * 8 + kc: l * 8 + kc + 1]

    def p_n2(self, l, kc):
        o = self.PD * 8
        return self.par[:, o + l * 8 + kc: o + l * 8 + kc + 1]

    def p_fn(self, kc):
        o = self.PD * 16
        return self.par[:, o + kc: o + kc + 1]

    def p_conv(self, l, k, c):
        o = self.PD * 16 + 8 + l * 24 + k * 8 + c
        return self.par[:, o:o + 1]

    def p_subln(self, l):
        o = self.PD * 16 + 8 + self.PD * 24 + l
        return self.par[:, o:o + 1]

    def p_lamc(self, l, k):
        o = self.PD * 16 + 8 + self.PD * 24 + self.PD + self.PD * 256 + 2 * l + k
        return self.par[:, o:o + 1]

    def p_lv(self, l):
        o = self.PD * 16 + 8 + self.PD * 24 + self.PD + l * 256
        return self.par[:, o:o + 256]

    def phase_init(self):
        dr = self.dr
        c32, par = self.c32, self.par
        self.emit("sp", lambda e: e.dma_start(out=c32, in_=dr["consts"]), newgen=[self.c32b], dma=True)
        if self.mode == "loop":
            self.load_params(False)
        else:
            self.emit("sp", lambda e: e.dma_start(out=par, in_=dr["params"]), newgen=[self.parb], dma=True)
        o32, o16, tri16 = self.ones32, self.ones16, self.tri16
        self.emit("dve", lambda e: e.memset(o32, 1.0), newgen=[self.ones32b])
        self.emit("dve", lambda e: e.memset(o16, 1.0), newgen=[self.ones16b])
        tri = self.cc("tri")
        self.emit("dve", lambda e: e.tensor_copy(out=tri16, in_=tri), reads=[self.c32b], newgen=[self.tri16b])
        self.phase_reset()
        P = self.P
        augt, augb = self.t16("augt", P)
        self.emit("sp", lambda e: e.dma_start(out=augt[0:32, :], in_=dr["aug"]), newgen=[augb], dma=True)
        dq, dk = dr["dq"], dr["dk"]
        self.emit("sp", lambda e: e.dma_start(out=dq[:, 64, :], in_=augt[0:16, :]), reads=[augb],
                  newgen=[self.db_aug], dma=True)
        self.emit("sp", lambda e: e.dma_start(out=dk[:, 64, :], in_=augt[16:32, :]), reads=[augb],
                  writes=[self.db_aug], dma=True)
        if self.mode != "loop":
            for l in range(self.DEPTH):
                self.lam_compute(l)
        self.barrier()

    def lam_compute(self, l):
        tmp, tmpb = self.lamtmp
        sab, sabb = self.lamsab
        lv = self.p_lv(l)
        tA, tB = tmp[:, 0:64], tmp[:, 64:128]
        self.emit("dve", lambda e: e.tensor_tensor(out=tA, in0=lv[:, 0:64], in1=lv[:, 64:128], op=ALU.mult),
                  reads=[self.parb], newgen=[tmpb])
        self.emit("dve", lambda e: e.tensor_tensor(out=tB, in0=lv[:, 128:192], in1=lv[:, 192:256], op=ALU.mult),
                  reads=[self.parb], writes=[tmpb])
        self.emit("dve", lambda e: e.reduce_sum(out=sab[:, 0:1], in_=tA, axis=mybir.AxisListType.X),
                  reads=[tmpb], newgen=[sabb])
        self.emit("dve", lambda e: e.reduce_sum(out=sab[:, 1:2], in_=tB, axis=mybir.AxisListType.X),
                  reads=[tmpb], writes=[sabb])
        self.emit("act", lambda e: e.activation(out=sab[:, 2:4], in_=sab[:, 0:2], func=AF.Exp),
                  reads=[sabb], writes=[sabb])
        lt = self.lamt[:, 2 * l:2 * l + 1]
        self.emit("dve", lambda e: e.tensor_tensor(out=lt, in0=sab[:, 3:4], in1=sab[:, 2:3], op=ALU.subtract),
                  reads=[sabb], newgen=[self.lamtb])
        nli = self.p_lamc(l, 0)
        self.emit("dve", lambda e: e.tensor_scalar(out=lt, in0=lt, scalar1=nli, scalar2=None, op0=ALU.add),
                  reads=[self.lamtb, self.parb], writes=[self.lamtb])

    def load_params(self, dyn):
        dr, par = self.dr, self.par
        if dyn:
            self.emit("sp", lambda e, ll: e.dma_start(out=par, in_=dr["params"][ll, :, :]), newgen=[self.parb], dma=True)
        else:
            self.emit("sp", lambda e: e.dma_start(out=par, in_=dr["params"][0, :, :]), newgen=[self.parb], dma=True)

    def psum_alloc(self, banks):
        st = {"i": 0}

        def nxt():
            b = banks[st["i"] % len(banks)]
            st["i"] += 1
            return b
        return nxt

    def rmsnorm(self, xs, xsb, w, wcol, hs, hsb, sq_tiles, rstd, stat_bank):
        ps = self.ps[stat_bank]
        psb = self.psb[stat_bank]
        o32 = self.ones32
        for kc in range(KC):
            sq, sqb = sq_tiles[kc % len(sq_tiles)]
            self.emit("act", lambda e, sq=sq, kc=kc: e.activation(out=sq[:, 0:w], in_=xs[:, kc, 0:w], func=AF.Square),
                      reads=[xsb], newgen=[sqb])
            self.emit("pe", lambda e, sq=sq, kc=kc: e.matmul(ps[:, 0:w], o32, sq[:, 0:w], start=(kc == 0), stop=(kc == KC - 1)),
                      reads=[sqb, self.ones32b], **({"newgen": [psb]} if kc == 0 else {"writes": [psb]}))
        rs, rsb = rstd
        self.emit("act", lambda e: e.activation(out=rs[:, 0:w], in_=ps[:, 0:w], func=AF.Ln, scale=1.0 / D, bias=EPS),
                  reads=[psb], newgen=[rsb])
        self.emit("act", lambda e: e.activation(out=rs[:, 0:w], in_=rs[:, 0:w], func=AF.Exp, scale=-0.5),
                  reads=[rsb], writes=[rsb])
        for kc in range(KC):
            eng = "dve"
            self.emit(eng, lambda e, kc=kc: e.scalar_tensor_tensor(out=hs[:, kc, 0:w], in0=xs[:, kc, 0:w], scalar=wcol(kc),
                                                                     in1=rs[:, 0:w], op0=ALU.mult, op1=ALU.mult),
                      reads=[xsb, rsb, self.parb], **({"newgen": [hsb]} if kc == 0 else {"writes": [hsb]}))

    def load_w(self, dst3, dstb, name, l, sl, first):
        kcn = dst3.shape[1]
        loop = self.mode == "loop"
        src = self.dr[name + "_cur"] if loop else self.dr[name]
        lay = 0 if loop else l
        s2 = src[lay][sl] if not isinstance(sl, int) else src[lay, sl]
        rd = [self.wstb[name]] if loop else []
        n = dst3.shape[2]
        step = 8 if n <= 3072 else 4
        for i, k0 in enumerate(range(0, kcn, step)):
            k1 = min(kcn, k0 + step)
            srcv = s2[k0 * 128:k1 * 128, :].rearrange("(a p) n -> p a n", p=128)
            self.emit("pool", lambda e, k0=k0, k1=k1, srcv=srcv: e.dma_start(out=dst3[:, k0:k1, :], in_=srcv),
                      reads=rd, **({"newgen": [dstb]} if (first and i == 0) else {"writes": [dstb]}), dma=True)

    def stage_weights(self):
        dr = self.dr
        plan = [("w_in", [(slice(None), slice(c, c + 3072)) for c in range(0, DIN, 3072)]),
                ("w_branch", [(b,) for b in range(3)]),
                ("w_out", [(slice(None),)]),
                ("w_up", [(slice(None), slice(c, c + 1024)) for c in range(0, DFF, 1024)]),
                ("w_down", [(slice(r, r + 1024),) for r in range(0, DFF, 1024)])]
        for name, parts in plan:
            b = self.wstb[name]
            for i, sl in enumerate(parts):
                def fn(e, ll, name=name, sl=sl):
                    return e.dma_start(out=dr[name + "_cur"][(0,) + tuple(sl)], in_=dr[name][(ll,) + tuple(sl)])
                self.emit("act", fn, **({"newgen": [b]} if i == 0 else {"writes": [b]}), dma=True)

    def phase0(self):
        self.phase_reset()
        dr = self.dr
        ident = self.cc("ident")
        xin = [self.t32(f"xin{i}", 1024) for i in range(2)]
        xs_t = [self.t32(f"xs{i}", 8 * 256, a=8) for i in range(2)]
        hs_t = [self.t16(f"hs{i}", 8 * 256, a=8) for i in range(2)]
        sq_t = [self.t32(f"sq{i}", 256) for i in range(2)]
        rstd = self.t32("rstd", 256)
        nxt = self.psum_alloc([0, 1, 2, 3])
        ib = 0
        for it, (c0, w) in enumerate(self.subtiles(256)):
            xs, xsb = xs_t[it % 2]
            hs, hsb = hs_t[it % 2]
            for j in range(w // 128):
                xi, xib = xin[ib % 2]
                ib += 1
                blk = (c0 // 128) + j
                if blk == 0:
                    self.emit("dve", lambda e, xi=xi: e.memset(xi, 0.0), newgen=[xib])
                    self.emit("sp", lambda e, xi=xi: e.dma_start(out=xi[PAD:128, :], in_=dr["meta"]), writes=[xib], dma=True)
                else:
                    r0 = (blk - 1) * 128
                    self.emit("sp", lambda e, xi=xi, r0=r0: e.dma_start(out=xi, in_=dr["x"][r0:r0 + 128, :]),
                              newgen=[xib], dma=True)
                for half in range(2):
                    bk = nxt()
                    ps, psb = self.ps[bk], self.psb[bk]
                    for q in range(4):
                        kc = half * 4 + q
                        self.emit("pe", lambda e, ps=ps, q=q, kc=kc, xi=xi: e.transpose(
                            out=ps[:, q * 128:(q + 1) * 128], in_=xi[:, kc * 128:(kc + 1) * 128], identity=ident),
                            reads=[xib, self.c32b], **({"newgen": [psb]} if q == 0 else {"writes": [psb]}))
                    eng = "act" if half == 0 else "dve"
                    dst = xs[:, half * 4:half * 4 + 4, j * 128:(j + 1) * 128]
                    src = ps.rearrange("p (a b) -> p a b", a=4)
                    first = (j == 0 and half == 0)
                    if eng == "act":
                        fn = lambda e, dst=dst, src=src: e.activation(out=dst, in_=src, func=AF.Copy)
                    else:
                        fn = lambda e, dst=dst, src=src: e.tensor_copy(out=dst, in_=src)
                    self.emit(eng, fn, reads=[psb], **({"newgen": [xsb]} if first else {"writes": [xsb]}))
            self.store_x_and_norm(xs, xsb, hs, hsb, c0, w, lambda kc: self.p_n1(0, kc), "hT", sq_t, rstd, 4)
        self.barrier()

    def store_x_and_norm(self, xs, xsb, hs, hsb, c0, w, wcol, hname, sq_t, rstd, stat_bank, store_x=True):
        dr = self.dr
        if store_x:
            self.emit("sp", lambda e: e.dma_start(out=dr["xT"][:, :, c0:c0 + w], in_=xs[:, :, 0:w]),
                      reads=[xsb], writes=[self.tb("xT", c0)], dma=True)
        self.rmsnorm(xs, xsb, w, wcol, hs, hsb, sq_t, rstd, stat_bank)
        self.emit("sp", lambda e: e.dma_start(out=dr[hname][:, :, c0:c0 + w], in_=hs[:, :, 0:w]),
                  reads=[hsb], writes=[self.tb(hname, c0)], dma=True)
    def mm_group(self, ps_ap, psb, pairs, extra_reads=()):
        n = len(pairs)
        for i, (lt, rh, bufs) in enumerate(pairs):
            self.emit("pe", lambda e, lt=lt, rh=rh, i=i: e.matmul(ps_ap, lt, rh, start=(i == 0), stop=(i == n - 1)),
                      reads=list(bufs) + list(extra_reads), **({"newgen": [psb]} if i == 0 else {"writes": [psb]}))

    def phase_p1a(self, l):
        self.barrier()
        self.phase_reset()
        dr = self.dr
        TW = 256
        W3 = self.WA[:, 0:8 * 3072].rearrange("p (a b) -> p a b", a=8)
        Wb = self.WAb
        self.load_w(W3, Wb, "w_in", l, (slice(None), slice(0, 3072)), True)
        if True:
            W3n = self.WB[:, 0:8 * 3072].rearrange("p (a b) -> p a b", a=8)
            self.load_w(W3n, self.WBb, "w_in", l, (slice(None), slice(3072, 6144)), True)
        hT = [self.t16(f"hT{i}", 8 * TW, a=8) for i in range(2)]
        q_t = self.t16("q", 4 * TW, a=4)
        qd_t = self.t16("qd", 4 * TW, a=4)
        k_t = self.t16("kT", 4 * TW, a=4)
        ktm_t = self.t16("ktm", 2 * 512, a=2)
        vtm_t = self.t16("vtm", 2 * 1024, a=2)
        g_t = self.t32("g", 8 * TW, a=8)
        qf_t = [self.t32(f"qf{i}", TW) for i in range(2)]
        scm_t = [self.t16(f"scm{i}", 128) for i in range(4)]
        st32 = [self.t32(f"st32_{h}", 256) for h in range(4)]
        st16 = [self.t16(f"st16_{h}", 256) for h in range(4)]
        o_t = [self.t32(f"o{i}", 2 * TW, a=2) for i in range(2)]
        sqo_t = [self.t32(f"sqo{i}", 2 * TW, a=2) for i in range(2)]
        mean_t = self.t32("mean", TW)
        msq_t = self.t32("msq", TW)
        rs_t = self.t32("rs", TW)
        tt_t = [self.t32(f"tt{i}", TW) for i in range(2)]
        ret_t = [self.t16(f"ret{i}", 8 * TW, a=8) for i in range(2)]
        for h in range(4):
            s32, s32b = st32[h]
            s16, s16b = st16[h]
            self.emit("dve", lambda e, s32=s32: e.memset(s32, 0.0), newgen=[s32b])
            self.emit("dve", lambda e, s16=s16: e.memset(s16, 0.0), newgen=[s16b])
        log_g, _ = retention_consts()
        s_decay = [float(np.exp(log_g[h] * 128.0)) for h in range(4)]
        intraT, qdec, kdec = self.cc("intraT"), self.cc("qdec"), self.cc("kdec")
        nxt = self.psum_alloc([0, 1, 2])
        PS_S, PS_O, PS_U, PS_ST = 3, (4, 5), 6, 7
        tiles = self.subtiles(TW)
        _tl = tiles
        for it, (c0, w) in enumerate(tiles):
            nb = w // 128
            ht, htb = hT[it % 2]
            if it == 0:
                self.pf_load(hT, "hT", _tl, 0)
            if it + 1 < len(_tl):
                self.pf_load(hT, "hT", _tl, it + 1)
            import os
            ksub = int(os.environ.get("K_SUB", "9"))
            if ksub <= 0:
                continue
            q, qb = q_t
            qd, qdb = qd_t
            kT, kTb = k_t
            g, gb = g_t
            ktm, ktmb = ktm_t
            vtm, vtmb = vtm_t

            def fm(col, ht=ht, htb=htb, w=w):
                bk = nxt()
                ps, psb = self.ps[bk], self.psb[bk]
                self.mm_group(ps[:, 0:w], psb, [(W3[:, kc, col:col + 128], ht[:, kc, 0:w], [Wb, htb]) for kc in range(KC)])
                return ps, psb
            for h in range(4):
                ps, psb = fm(C_RQ + h * 128)
                qf, qfb = qf_t[h % 2]
                self.emit("act", lambda e, ps=ps, qf=qf, w=w: e.activation(out=qf[:, 0:w], in_=ps[:, 0:w], func=AF.Copy),
                          reads=[psb], newgen=[qfb])
                self.emit("pool", lambda e, qf=qf, h=h, w=w: e.tensor_copy(out=q[:, h, 0:w], in_=qf[:, 0:w]),
                          reads=[qfb], **({"newgen": [qb]} if h == 0 else {"writes": [qb]}))
                self.emit("dve", lambda e, qf=qf, h=h, w=w, nb=nb: e.tensor_tensor(
                    out=qd[:, h, 0:w].rearrange("p (a b) -> p a b", b=128),
                    in0=qf[:, 0:w].rearrange("p (a b) -> p a b", b=128),
                    in1=qdec[:, h * 128:(h + 1) * 128].unsqueeze(1).broadcast_to([128, nb, 128]), op=ALU.mult),
                    reads=[qfb, self.c32b], **({"newgen": [qdb]} if h == 0 else {"writes": [qdb]}))
            if ksub == 1 and os.environ.get("K_SUB2") == "a":
                continue
            for h in range(4):
                ps, psb = fm(C_RK + h * 128)
                self.emit("act", lambda e, ps=ps, h=h, w=w: e.activation(out=kT[:, h, 0:w], in_=ps[:, 0:w], func=AF.Copy,
                                                                         scale=RDK ** -0.5),
                          reads=[psb], **({"newgen": [kTb]} if h == 0 else {"writes": [kTb]}))
            for c in range(8):
                ps, psb = fm(C_RG + c * 128)
                self.emit("act", lambda e, ps=ps, c=c, w=w: e.activation(out=g[:, c, 0:w], in_=ps[:, 0:w], func=AF.Silu),
                          reads=[psb], **({"newgen": [gb]} if c == 0 else {"writes": [gb]}))
            if ksub == 1 and os.environ.get("K_SUB2") == "b":
                continue
            for j in range(nb):
                bk = nxt()
                ps, psb = self.ps[bk], self.psb[bk]
                self.mm_group(ps[:, 0:512], psb, [(ht[:, kc, j * 128:(j + 1) * 128], W3[:, kc, C_RK:C_RK + 512], [Wb, htb])
                                                  for kc in range(KC)])
                self.emit("dve", lambda e, ps=ps, j=j: e.tensor_tensor(out=ktm[:, j, :], in0=ps[:, 0:512], in1=kdec, op=ALU.mult),
                          reads=[psb, self.c32b], **({"newgen": [ktmb]} if j == 0 else {"writes": [ktmb]}))
                for half in range(2):
                    bk = nxt()
                    ps, psb = self.ps[bk], self.psb[bk]
                    self.mm_group(ps[:, 0:512], psb, [(ht[:, kc, j * 128:(j + 1) * 128],
                                                       W3[:, kc, C_RV + half * 512:C_RV + half * 512 + 512], [Wb, htb])
                                                      for kc in range(KC)])
                    self.emit("act", lambda e, ps=ps, j=j, half=half: e.activation(
                        out=vtm[:, j, half * 512:(half + 1) * 512], in_=ps[:, 0:512], func=AF.Copy),
                        reads=[psb], **({"newgen": [vtmb]} if (j == 0 and half == 0) else {"writes": [vtmb]}))
            rt, rtb = ret_t[it % 2]
            import os
            ksub = int(os.environ.get("K_SUB", "9"))
            if ksub <= 1:
                continue
            first_ret = True
            for hg in range(2):
                heads = (2 * hg, 2 * hg + 1)
                for j in range(nb):
                    cs = slice(j * 128, (j + 1) * 128)
                    pss, pssb = self.ps[PS_S], self.psb[PS_S]
                    for ih, h in enumerate(heads):
                        self.emit("pe", lambda e, h=h, ih=ih, cs=cs: e.matmul(pss[:, ih * 128:(ih + 1) * 128], kT[:, h, cs], q[:, h, cs],
                                                                             start=True, stop=True),
                                  reads=[kTb, qb], **({"newgen": [pssb]} if ih == 0 else {"writes": [pssb]}))
                    for ih, h in enumerate(heads):
                        sc, scb = scm_t[h]
                        self.emit("dve", lambda e, sc=sc, ih=ih, h=h: e.tensor_tensor(
                            out=sc, in0=pss[:, ih * 128:(ih + 1) * 128], in1=intraT[:, h * 128:(h + 1) * 128], op=ALU.mult),
                            reads=[pssb, self.c32b], newgen=[scb])
                    pu, pub = self.ps[PS_U], self.psb[PS_U]
                    for ih, h in enumerate(heads):
                        sc, scb = scm_t[h]
                        s16, s16b = st16[h]
                        po, pob = self.ps[PS_O[ih]], self.psb[PS_O[ih]]
                        for vh in range(2):
                            oc = slice(vh * TW + j * 128, vh * TW + (j + 1) * 128)
                            vcol = slice(h * 256 + vh * 128, h * 256 + (vh + 1) * 128)
                            ng = (j == 0 and vh == 0)
                            self.emit("pe", lambda e, po=po, oc=oc, vcol=vcol, j=j, sc=sc: e.matmul(
                                po[:, oc], vtm[:, j, vcol], sc, start=True, stop=False),
                                reads=[vtmb, scb], **({"newgen": [pob]} if ng else {"writes": [pob]}))
                            self.emit("pe", lambda e, po=po, oc=oc, vh=vh, h=h, cs=cs, s16=s16: e.matmul(
                                po[:, oc], s16[:, vh * 128:(vh + 1) * 128], qd[:, h, cs], start=False, stop=True),
                                reads=[s16b, qdb], writes=[pob])
                        self.emit("pe", lambda e, ih=ih, h=h, j=j: e.matmul(
                            pu[:, ih * 256:(ih + 1) * 256], ktm[:, j, h * 128:(h + 1) * 128], vtm[:, j, h * 256:(h + 1) * 256],
                            start=True, stop=True),
                            reads=[ktmb, vtmb], **({"newgen": [pub]} if ih == 0 else {"writes": [pub]}))
                    for ih, h in enumerate(heads):
                        s32, s32b = st32[h]
                        s16, s16b = st16[h]
                        self.emit("dve", lambda e, s32=s32, ih=ih, h=h: e.scalar_tensor_tensor(
                            out=s32, in0=s32, scalar=s_decay[h], in1=pu[:, ih * 256:(ih + 1) * 256], op0=ALU.mult, op1=ALU.add),
                            reads=[pub], newgen=[s32b])
                        self.emit("act", lambda e, s32=s32, s16=s16: e.activation(out=s16, in_=s32, func=AF.Copy),
                                  reads=[s32b], newgen=[s16b])
                for ih, h in enumerate(heads):
                    if ksub <= 2:
                        continue
                    po, pob = self.ps[PS_O[ih]], self.psb[PS_O[ih]]
                    o, ob = o_t[ih]
                    sqo, sqob = sqo_t[ih]
                    po3 = po[:, 0:2 * TW].rearrange("p (a b) -> p a b", a=2)
                    self.emit("act", lambda e, o=o, po3=po3, w=w: e.activation(out=o[:, :, 0:w], in_=po3[:, :, 0:w], func=AF.Copy),
                              reads=[pob], newgen=[ob])
                    self.emit("act", lambda e, sqo=sqo, po3=po3, w=w: e.activation(out=sqo[:, :, 0:w], in_=po3[:, :, 0:w], func=AF.Square),
                              reads=[pob], newgen=[sqob])
                    pst, pstb = self.ps[PS_ST], self.psb[PS_ST]
                    o32 = self.ones32
                    self.mm_group(pst[:, 0:w], pstb, [(o32, o[:, vh, 0:w], [ob, self.ones32b]) for vh in range(2)])
                    for vh in range(2):
                        self.emit("pe", lambda e, sqo=sqo, vh=vh, w=w: e.matmul(pst[:, TW:TW + w], o32, sqo[:, vh, 0:w],
                                                                              start=(vh == 0), stop=(vh == 1)),
                                  reads=[sqob, self.ones32b], writes=[pstb])
                    mean, meanb = mean_t
                    msq, msqb = msq_t
                    rs, rsb = rs_t
                    self.emit("act", lambda e, w=w: e.activation(out=mean[:, 0:w], in_=pst[:, 0:w], func=AF.Copy, scale=1.0 / RDV),
                              reads=[pstb], newgen=[meanb])
                    self.emit("dve", lambda e, w=w: e.tensor_tensor(out=msq[:, 0:w], in0=mean[:, 0:w], in1=mean[:, 0:w], op=ALU.mult),
                              reads=[meanb], newgen=[msqb])
                    self.emit("dve", lambda e, w=w: e.scalar_tensor_tensor(out=rs[:, 0:w], in0=pst[:, TW:TW + w], scalar=1.0 / RDV,
                                                                           in1=msq[:, 0:w], op0=ALU.mult, op1=ALU.subtract),
                              reads=[pstb, msqb], newgen=[rsb])
                    self.emit("act", lambda e, w=w: e.activation(out=rs[:, 0:w], in_=rs[:, 0:w], func=AF.Ln, scale=1.0, bias=EPS),
                              reads=[rsb], writes=[rsb])
                    self.emit("act", lambda e, w=w: e.activation(out=rs[:, 0:w], in_=rs[:, 0:w], func=AF.Exp, scale=-0.5),
                              reads=[rsb], writes=[rsb])
                    for vh in range(2):
                        tt, ttb = tt_t[vh]
                        c = h * 2 + vh
                        self.emit("dve", lambda e, tt=tt, o=o, vh=vh, w=w: e.tensor_tensor(out=tt[:, 0:w], in0=o[:, vh, 0:w],
                                                                                         in1=mean[:, 0:w], op=ALU.subtract),
                                  reads=[ob, meanb], newgen=[ttb])
                        self.emit("pool", lambda e, tt=tt, w=w: e.tensor_tensor(out=tt[:, 0:w], in0=tt[:, 0:w], in1=rs[:, 0:w], op=ALU.mult),
                                  reads=[ttb, rsb], writes=[ttb])
                        self.emit("pool", lambda e, tt=tt, c=c, w=w, rt=rt: e.tensor_tensor(out=rt[:, c, 0:w], in0=tt[:, 0:w], in1=g[:, c, 0:w],
                                                                                          op=ALU.mult),
                                  reads=[ttb, gb], **({"newgen": [rtb]} if first_ret else {"writes": [rtb]}))
                        first_ret = False
            if ksub <= 2:
                continue
            self.emit("sp", lambda e, rt=rt, c0=c0, w=w: e.dma_start(out=dr["retT"][:, :, c0:c0 + w], in_=rt[:, :, 0:w]),
                      reads=[rtb], writes=[self.tb("retT", c0)], dma=True)
    def phase_p1b(self, l):
        self.barrier()
        self.phase_reset()
        dr = self.dr
        TW = 512
        W3 = self.WB[:, 0:8 * 3072].rearrange("p (a b) -> p a b", a=8)
        Wb = self.WBb
        W3n = self.WA[:, 0:8 * 3072].rearrange("p (a b) -> p a b", a=8)
        self.load_w(W3n, self.WAb, "w_in", l, (slice(None), slice(6144, 9216)), True)
        hT = [self.t16(f"hT{i}", 8 * TW, a=8) for i in range(2)]
        stg = [self.t16(f"stg{i}", TW) for i in range(4)]
        vst = [self.t16(f"vst{i}", 1024) for i in range(2)]
        nxt = self.psum_alloc([0, 1, 2, 3, 4, 5])
        si = 0
        vi = 0
        _tl = self.subtiles(TW)
        for it, (c0, w) in enumerate(_tl):
            nb = w // 128
            ht, htb = hT[it % 2]
            if it == 0:
                self.pf_load(hT, "hT", _tl, 0)
            if it + 1 < len(_tl):
                self.pf_load(hT, "hT", _tl, it + 1)
            for which, cbase, scale in (("dq", 0, DHD ** -0.5), ("dk", 1024, 1.0)):
                for h in range(8):
                    bk = nxt()
                    ps, psb = self.ps[bk], self.psb[bk]
                    col = cbase + h * 128
                    self.mm_group(ps[:, 0:w], psb, [(W3[:, kc, col:col + 128], ht[:, kc, 0:w], [Wb, htb]) for kc in range(KC)])
                    st, stb = stg[si % 4]
                    si += 1
                    eng = "act" if h % 2 == 0 else "dve"
                    if eng == "act":
                        fn = lambda e, st=st, ps=ps, w=w, scale=scale: e.activation(out=st[:, 0:w], in_=ps[:, 0:w], func=AF.Copy, scale=scale)
                    else:
                        fn = lambda e, st=st, ps=ps, w=w, scale=scale: e.tensor_scalar(out=st[:, 0:w], in0=ps[:, 0:w], scalar1=scale,
                                                                                     scalar2=None, op0=ALU.mult)
                    self.emit(eng, fn, reads=[psb], newgen=[stb])
                    for m in range(2):
                        self.emit("sp", lambda e, st=st, which=which, h=h, m=m, c0=c0, w=w: e.dma_start(
                            out=dr[which][2 * h + m, 0:64, c0:c0 + w], in_=st[m * 64:(m + 1) * 64, 0:w]),
                            reads=[stb], writes=[self.tb(which, c0)], dma=True)
            for j in range(nb):
                vs, vsb = vst[vi % 2]
                vi += 1
                for half in range(2):
                    bk = nxt()
                    ps, psb = self.ps[bk], self.psb[bk]
                    cb = 2048 + half * 512
                    self.mm_group(ps[:, 0:512], psb, [(ht[:, kc, j * 128:(j + 1) * 128], W3[:, kc, cb:cb + 512], [Wb, htb])
                                                      for kc in range(KC)])
                    eng = "act" if half == 0 else "dve"
                    dst = vs[:, half * 512:(half + 1) * 512]
                    if eng == "act":
                        fn = lambda e, dst=dst, ps=ps: e.activation(out=dst, in_=ps[:, 0:512], func=AF.Copy)
                    else:
                        fn = lambda e, dst=dst, ps=ps: e.tensor_copy(out=dst, in_=ps[:, 0:512])
                    self.emit(eng, fn, reads=[psb], **({"newgen": [vsb]} if half == 0 else {"writes": [vsb]}))
                r0 = c0 + j * 128
                self.emit("sp", lambda e, vs=vs, r0=r0: e.dma_start(out=dr["dv"][r0:r0 + 128, :], in_=vs),
                          reads=[vsb], writes=[self.tb("dv", c0)], dma=True)

    def phase_p1c(self, l):
        self.barrier()
        self.phase_reset()
        dr = self.dr
        TW = 512
        W3 = self.WA[:, 0:8 * 3072].rearrange("p (a b) -> p a b", a=8)
        Wb = self.WAb
        hT = [self.t16(f"hT{i}", 8 * TW, a=8) for i in range(2)]
        cxs = [self.t32(f"cxs{i}", TW) for i in range(2)]
        us = [self.t32(f"u{i}", TW + 2) for i in range(2)]
        t1s = [self.t32(f"t1{i}", TW) for i in range(2)]
        halo, halob = self.t32("halo", 16, a=8)
        cv = [self.t16(f"cv{i}", 8 * TW, a=8) for i in range(2)]
        self.emit("dve", lambda e: e.memset(halo, 0.0), newgen=[halob])
        nxt = self.psum_alloc([0, 1, 2, 3, 4, 5])
        ci = 0
        _tl = self.subtiles(TW)
        for it, (c0, w) in enumerate(_tl):
            ht, htb = hT[it % 2]
            if it == 0:
                self.pf_load(hT, "hT", _tl, 0)
            if it + 1 < len(_tl):
                self.pf_load(hT, "hT", _tl, it + 1)
            co, cob = cv[it % 2]
            for c in range(8):
                pss = []
                for grp in range(3):
                    bk = nxt()
                    ps, psb = self.ps[bk], self.psb[bk]
                    col = grp * 1024 + c * 128
                    self.mm_group(ps[:, 0:w], psb, [(W3[:, kc, col:col + 128], ht[:, kc, 0:w], [Wb, htb]) for kc in range(KC)])
                    pss.append((ps, psb))
                (pb, pbb), (pc, pcb), (px, pxb) = pss
                cx, cxb = cxs[ci % 2]
                u, ub = us[ci % 2]
                t1, t1b = t1s[ci % 2]
                ci += 1
                self.emit("act", lambda e, cx=cx, px=px, w=w: e.activation(out=cx[:, 0:w], in_=px[:, 0:w], func=AF.Copy),
                          reads=[pxb], newgen=[cxb])
                self.emit("pool", lambda e, u=u, c=c: e.tensor_copy(out=u[:, 0:2], in_=halo[:, c, :]), reads=[halob], newgen=[ub])
                self.emit("dve", lambda e, u=u, pc=pc, cx=cx, w=w: e.tensor_tensor(out=u[:, 2:2 + w], in0=pc[:, 0:w], in1=cx[:, 0:w], op=ALU.mult),
                          reads=[pcb, cxb], writes=[ub])
                self.emit("pool", lambda e, u=u, c=c, w=w: e.tensor_copy(out=halo[:, c, :], in_=u[:, w:w + 2]), reads=[ub], writes=[halob])
                w0, w1, w2 = self.p_conv(l, 0, c), self.p_conv(l, 1, c), self.p_conv(l, 2, c)
                self.emit("pool", lambda e, t1=t1, u=u, w0=w0, w=w: e.tensor_scalar(out=t1[:, 0:w], in0=u[:, 0:w], scalar1=w0, scalar2=None,
                                                                                   op0=ALU.mult), reads=[ub, self.parb], newgen=[t1b])
                self.emit("dve", lambda e, t1=t1, u=u, w1=w1, w=w: e.scalar_tensor_tensor(out=t1[:, 0:w], in0=u[:, 1:1 + w], scalar=w1,
                                                                                          in1=t1[:, 0:w], op0=ALU.mult, op1=ALU.add),
                          reads=[ub, t1b, self.parb], writes=[t1b])
                self.emit("dve", lambda e, t1=t1, u=u, w2=w2, w=w: e.scalar_tensor_tensor(out=t1[:, 0:w], in0=u[:, 2:2 + w], scalar=w2,
                                                                                          in1=t1[:, 0:w], op0=ALU.mult, op1=ALU.add),
                          reads=[ub, t1b, self.parb], writes=[t1b])
                self.emit("dve", lambda e, co=co, c=c, pb=pb, t1=t1, w=w: e.tensor_tensor(out=co[:, c, 0:w], in0=pb[:, 0:w], in1=t1[:, 0:w],
                                                                                         op=ALU.mult),
                          reads=[pbb, t1b], **({"newgen": [cob]} if c == 0 else {"writes": [cob]}))
            self.emit("sp", lambda e, co=co, c0=c0, w=w: e.dma_start(out=dr["convT"][:, :, c0:c0 + w], in_=co[:, :, 0:w]),
                      reads=[cob], writes=[self.tb("convT", c0)], dma=True)

    def phase_att(self, l):
        self.barrier()
        self.phase_reset()
        dr = self.dr
        P, NB, ND = self.P, self.NB, self.ND
        TW = 512
        slots = []
        for s, (Wt, Wtb) in enumerate(((self.WA, self.WAb), (self.WB, self.WBb))):
            kA = [Wt[:, m * P:(m + 1) * P] for m in range(2)]
            V = Wt[:, 2 * P:3 * P].rearrange("p (a b) -> p a b", b=128)
            slots.append((kA, V, Wtb))
        assert 3 * P <= 32768
        qt = [[self.t16(f"q{s}{m}", TW) for m in range(2)] for s in range(2)]
        pts = [self.t16(f"pT{i}", TW) for i in range(4)]
        rc = [self.t32(f"rc{m}", TW) for m in range(2)]
        tm = [self.t32(f"tm{m}", TW) for m in range(2)]
        dd, ddb = self.t32("dd", TW)
        sq, sqb = self.t32("sq", TW)
        rs, rsb = self.t32("rs", TW)
        ost = [self.t16(f"ost{i}", TW) for i in range(2)]
        alibi, alibi0 = self.cc("alibi"), self.cc("alibi0")
        nxt = self.psum_alloc([0, 1, 2])
        PS_O, PS_SUM, PS_ST = (3, 4), (5, 6), 7
        neglam = self.lamt[:, 2 * l:2 * l + 1]
        subw = self.p_subln(l)
        post = 1.0 - layer_lam_init(l)
        tiles = self.subtiles(TW)
        pi = 0
        qi = 0
        oi = 0
        def load_kv(h):
            kA, V, Wtb = slots[h % 2]
            for m in range(2):
                self.emit("sp", lambda e, kA=kA, m=m, h=h: e.dma_start(out=kA[m][0:65, :], in_=dr["dk"][2 * h + m, :, :]),
                          reads=self.db["dk"] + [self.db_aug], **({"newgen": [Wtb]} if m == 0 else {"writes": [Wtb]}), dma=True)
            nsp = 8
            for part in range(nsp):
                b0 = (NB * part) // nsp
                b1 = (NB * (part + 1)) // nsp
                if b1 > b0:
                    src = dr["dv"][b0 * 128:b1 * 128, h * 128:(h + 1) * 128].rearrange("(a p) c -> p a c", p=128)
                    self.emit("sp", lambda e, V=V, b0=b0, b1=b1, src=src: e.dma_start(out=V[:, b0:b1, :], in_=src),
                              reads=self.db["dv"], writes=[Wtb], dma=True)
        def ldq(idx):
            hh, TT = divmod(idx, len(tiles))
            cc0, ww = tiles[TT]
            for m in range(2):
                qa, qab = qt[idx % 2][m]
                self.emit("sp", lambda e, qa=qa, m=m, hh=hh, cc0=cc0, ww=ww: e.dma_start(
                    out=qa[0:65, 0:ww], in_=dr["dq"][2 * hh + m, :, cc0:cc0 + ww]),
                    reads=[self.tb("dq", cc0), self.db_aug], newgen=[qab], dma=True)
        load_kv(0)
        for h in range(DH):
            kA, V, Wtb = slots[h % 2]
            if h + 1 < DH:
                load_kv(h + 1)
            for T, (c0, w) in enumerate(tiles):
                qs = qt[qi % 2]
                if qi == 0:
                    ldq(0)
                if qi + 1 < DH * len(tiles):
                    ldq(qi + 1)
                qi += 1
                kb_first = c0 // 128
                kmax = kb_first + w // 128 - 1
                for kb in range(kmax + 1):
                    a_k = kb - kb_first
                    diag = a_k >= 0
                    col0 = 128 * a_k if diag else 0
                    ddi = (kb_first - kb) + 3
                    tab = alibi0 if kb == 0 else alibi
                    bias = tab[:, h * ND + ddi: h * ND + ddi + 1]
                    for m in range(2):
                        qa, qab = qs[m]
                        bk = nxt()
                        ps, psb = self.ps[bk], self.psb[bk]
                        self.emit("pe", lambda e, ps=ps, kA=kA, m=m, kb=kb, qa=qa, col0=col0, w=w: e.matmul(
                            ps[:, col0:w], kA[m][0:65, kb * 128:(kb + 1) * 128], qa[0:65, col0:w], start=True, stop=True),
                            reads=[Wtb, qab], newgen=[psb])
                        pt, ptb = pts[pi % 4]
                        pi += 1
                        self.emit("act", lambda e, pt=pt, ps=ps, col0=col0, w=w, bias=bias: e.activation(
                            out=pt[:, col0:w], in_=ps[:, col0:w], func=AF.Exp, bias=bias, scale=1.0),
                            reads=[psb, self.c32b], newgen=[ptb])
                        if diag:
                            self.emit("pool", lambda e, pt=pt, col0=col0: e.tensor_tensor(
                                out=pt[:, col0:col0 + 128], in0=pt[:, col0:col0 + 128], in1=self.tri16, op=ALU.mult),
                                reads=[ptb, self.tri16b], writes=[ptb])
                        po, pob = self.ps[PS_O[m]], self.psb[PS_O[m]]
                        psu, psub = self.ps[PS_SUM[m]], self.psb[PS_SUM[m]]
                        self.emit("pe", lambda e, po=po, V=V, kb=kb, pt=pt, col0=col0, w=w, kmax=kmax: e.matmul(
                            po[:, col0:w], V[:, kb, :], pt[:, col0:w], start=(kb == 0), stop=(kb == kmax)),
                            reads=[Wtb, ptb], **({"newgen": [pob]} if kb == 0 else {"writes": [pob]}))
                        self.emit("pe", lambda e, psu=psu, pt=pt, col0=col0, w=w, kb=kb, kmax=kmax: e.matmul(
                            psu[:, col0:w], self.ones16, pt[:, col0:w], start=(kb == 0), stop=(kb == kmax)),
                            reads=[self.ones16b, ptb], **({"newgen": [psub]} if kb == 0 else {"writes": [psub]}))
                for m in range(2):
                    r, rb = rc[m]
                    t, tb_ = tm[m]
                    psu, psub = self.ps[PS_SUM[m]], self.psb[PS_SUM[m]]
                    po, pob = self.ps[PS_O[m]], self.psb[PS_O[m]]
                    self.emit("dve", lambda e, r=r, psu=psu, w=w: e.tensor_scalar(out=r[:, 0:w], in0=psu[:, 0:w], scalar1=1e-30, scalar2=None,
                                                                                 op0=ALU.add), reads=[psub], newgen=[rb])
                    self.emit("dve", lambda e, r=r, w=w: e.reciprocal(out=r[:, 0:w], in_=r[:, 0:w]), reads=[rb], writes=[rb])
                    self.emit("dve", lambda e, t=t, po=po, r=r, w=w: e.tensor_tensor(out=t[:, 0:w], in0=po[:, 0:w], in1=r[:, 0:w], op=ALU.mult),
                              reads=[pob, rb], newgen=[tb_])
                self.emit("dve", lambda e, w=w: e.scalar_tensor_tensor(out=dd[:, 0:w], in0=tm[1][0][:, 0:w], scalar=neglam,
                                                                       in1=tm[0][0][:, 0:w], op0=ALU.mult, op1=ALU.add),
                          reads=[tm[0][1], tm[1][1], self.lamtb], newgen=[ddb])
                self.emit("act", lambda e, w=w: e.activation(out=sq[:, 0:w], in_=dd[:, 0:w], func=AF.Square), reads=[ddb], newgen=[sqb])
                pst, pstb = self.ps[PS_ST], self.psb[PS_ST]
                self.emit("pe", lambda e, w=w: e.matmul(pst[:, 0:w], self.ones32, sq[:, 0:w], start=True, stop=True),
                          reads=[self.ones32b, sqb], newgen=[pstb])
                self.emit("act", lambda e, w=w: e.activation(out=rs[:, 0:w], in_=pst[:, 0:w], func=AF.Ln, scale=1.0 / DVD, bias=EPS),
                          reads=[pstb], newgen=[rsb])
                self.emit("act", lambda e, w=w: e.activation(out=rs[:, 0:w], in_=rs[:, 0:w], func=AF.Exp, scale=-0.5,
                                                             bias=self.p_lamc(l, 1)), reads=[rsb, self.parb], writes=[rsb])
                os_, osb = ost[oi % 2]
                oi += 1
                self.emit("dve", lambda e, os_=os_, w=w: e.scalar_tensor_tensor(out=os_[:, 0:w], in0=dd[:, 0:w], scalar=subw, in1=rs[:, 0:w],
                                                                                 op0=ALU.mult, op1=ALU.mult),
                          reads=[ddb, rsb, self.parb], newgen=[osb])
                self.emit("sp", lambda e, os_=os_, h=h, c0=c0, w=w: e.dma_start(out=dr["diffT"][:, h, c0:c0 + w], in_=os_[:, 0:w]),
                          reads=[osb], writes=[self.tb("diffT", c0)], dma=True)
    def pf_load(self, bufs, name, tl, it):
        c0, w = tl[it]
        ht, htb = bufs[it % 2]
        dr = self.dr
        self.emit("sp", lambda e: e.dma_start(out=ht[:, :, 0:w], in_=dr[name][:, :, c0:c0 + w]),
                  reads=[self.tb(name, c0)], newgen=[htb], dma=True)

    def zero_pad_cols(self, xs, xsb):
        self.emit("dve", lambda e: e.memset(xs[:, :, 0:PAD], 0.0), reads=[xsb], writes=[xsb])

    def phase_p2(self, l):
        self.barrier()
        self.phase_reset()
        dr = self.dr
        TW = 256
        Wg = self.WB[:, 0:8 * 3072].rearrange("p (a b) -> p a b", a=8)
        Wo = self.WB[:, 8 * 3072:8 * 4096].rearrange("p (a b) -> p a b", a=8)
        Wbr = self.WA[:, 0:24 * 1024].rearrange("p (a b) -> p a b", a=24)
        self.load_w(Wg, self.WBb, "w_in", l, (slice(None), slice(C_G, C_G + 3072)), True)
        for b in range(3):
            self.load_w(Wbr[:, b * 8:(b + 1) * 8, :], self.WAb, "w_branch", l, b, b == 0)
        self.load_w(Wo, self.WBb, "w_out", l, (slice(None), slice(None)), False)
        hT = [self.t16(f"hT{i}", 8 * TW, a=8) for i in range(2)]
        brs = [self.t16(f"br{b}", 8 * TW, a=8) for b in range(3)]
        mg, mgb = self.t16("mg", 8 * TW, a=8)
        gs = [self.t32(f"g{b}", TW) for b in range(3)]
        mts = [self.t32(f"mt{i}", TW) for i in range(2)]
        xs, xsb = self.t32("xs", 8 * TW, a=8)
        h2 = [self.t16(f"h2{i}", 8 * TW, a=8) for i in range(2)]
        sq_t = [self.t32(f"sq{i}", TW) for i in range(2)]
        rstd = self.t32("rstd", TW)
        nxt = self.psum_alloc([0, 1, 2, 3, 4, 5, 6])
        names = ("retT", "diffT", "convT")
        _tl = self.subtiles(TW)
        for it, (c0, w) in enumerate(_tl):
            ht, htb = hT[it % 2]
            if it == 0:
                self.pf_load(hT, "hT", _tl, 0)
            if it + 1 < len(_tl):
                self.pf_load(hT, "hT", _tl, it + 1)
            for b in range(3):
                bt, btb = brs[b]
                self.emit("sp", lambda e, bt=bt, b=b, c0=c0, w=w: e.dma_start(out=bt[:, :, 0:w], in_=dr[names[b]][:, :, c0:c0 + w]),
                          reads=[self.tb(names[b], c0)], newgen=[btb], dma=True)
            self.emit("sp", lambda e, c0=c0, w=w: e.dma_start(out=xs[:, :, 0:w], in_=dr["xT"][:, :, c0:c0 + w]),
                      reads=[self.tb("xT", c0)], newgen=[xsb], dma=True)
            for n in range(8):
                pbs = []
                for b in range(3):
                    bk = nxt()
                    pg_, pgb = self.ps[bk], self.psb[bk]
                    col = b * 1024 + n * 128
                    self.mm_group(pg_[:, 0:w], pgb, [(Wg[:, kc, col:col + 128], ht[:, kc, 0:w], [self.WBb, htb]) for kc in range(KC)])
                    g, gb = gs[b]
                    self.emit("act", lambda e, g=g, pg_=pg_, w=w: e.activation(out=g[:, 0:w], in_=pg_[:, 0:w], func=AF.Sigmoid),
                              reads=[pgb], newgen=[gb])
                    bk = nxt()
                    pb, pbb = self.ps[bk], self.psb[bk]
                    bt, btb = brs[b]
                    self.mm_group(pb[:, 0:w], pbb, [(Wbr[:, b * 8 + kc, n * 128:(n + 1) * 128], bt[:, kc, 0:w], [self.WAb, btb])
                                                    for kc in range(KC)])
                    pbs.append((pb, pbb))
                m0, m0b = mts[0]
                m1, m1b = mts[1]
                self.emit("dve", lambda e, w=w, pb=pbs[0][0], g=gs[0][0]: e.tensor_tensor(out=m0[:, 0:w], in0=pb[:, 0:w], in1=g[:, 0:w], op=ALU.mult),
                          reads=[pbs[0][1], gs[0][1]], newgen=[m0b])
                self.emit("dve", lambda e, w=w, pb=pbs[1][0], g=gs[1][0]: e.tensor_tensor(out=m1[:, 0:w], in0=pb[:, 0:w], in1=g[:, 0:w], op=ALU.mult),
                          reads=[pbs[1][1], gs[1][1]], newgen=[m1b])
                self.emit("pool", lambda e, w=w: e.tensor_tensor(out=m0[:, 0:w], in0=m0[:, 0:w], in1=m1[:, 0:w], op=ALU.add),
                          reads=[m0b, m1b], writes=[m0b])
                self.emit("dve", lambda e, w=w, pb=pbs[2][0], g=gs[2][0]: e.tensor_tensor(out=m1[:, 0:w], in0=pb[:, 0:w], in1=g[:, 0:w], op=ALU.mult),
                          reads=[pbs[2][1], gs[2][1], m0b], newgen=[m1b])
                self.emit("pool", lambda e, w=w, n=n: e.tensor_tensor(out=mg[:, n, 0:w], in0=m0[:, 0:w], in1=m1[:, 0:w], op=ALU.add),
                          reads=[m0b, m1b], **({"newgen": [mgb]} if n == 0 else {"writes": [mgb]}))
            for n2 in range(8):
                bk = nxt()
                ps, psb = self.ps[bk], self.psb[bk]
                self.mm_group(ps[:, 0:w], psb, [(Wo[:, n, n2 * 128:(n2 + 1) * 128], mg[:, n, 0:w], [self.WBb, mgb]) for n in range(8)])
                self.emit("dve", lambda e, ps=ps, n2=n2, w=w: e.tensor_tensor(out=xs[:, n2, 0:w], in0=ps[:, 0:w], in1=xs[:, n2, 0:w], op=ALU.add),
                          reads=[psb, xsb], writes=[xsb])
            if c0 == 0:
                self.zero_pad_cols(xs, xsb)
            hs, hsb = h2[it % 2]
            self.store_x_and_norm(xs, xsb, hs, hsb, c0, w, lambda kc: self.p_n2(l, kc), "h2T", sq_t, rstd, 7)

    def phase_mlp(self, l):
        self.barrier()
        self.phase_reset()
        dr = self.dr
        TW = 256
        last = (l == self.DEPTH - 1) and self.mode == "fused"
        layer_mode = self.mode in ("layer", "loop")
        Wu = self.WA.rearrange("p (a b) -> p a b", a=8)
        Wd = self.WB.rearrange("p (a b) -> p a b", a=32)
        self.load_w(Wu, self.WAb, "w_up", l, (slice(None), slice(None)), True)
        self.load_w(Wd, self.WBb, "w_down", l, (slice(None), slice(None)), True)
        h2 = [self.t16(f"h2{i}", 8 * TW, a=8) for i in range(2)]
        xs, xsb = self.t32("xs", 8 * TW, a=8)
        u, ub = self.t16("u", 32 * TW, a=32)
        rts = [self.t32(f"r{i}", TW) for i in range(2)]
        sq_t = [self.t32(f"sq{i}", TW) for i in range(2)]
        rstd = self.t32("rstd", TW)
        if not last:
            hn = [self.t16(f"hn{i}", 8 * TW, a=8) for i in range(2)]
        else:
            ys, ysb = self.t32("ys", 8 * TW, a=8)
            ytm = [self.t32(f"ytm{i}", 1024) for i in range(2)]
            ident = self.cc("ident")
        nxt = self.psum_alloc([0, 1, 2, 3, 4, 5, 6])
        yi = 0
        _tl = self.subtiles(TW)
        for it, (c0, w) in enumerate(_tl):
            ht, htb = h2[it % 2]
            if it == 0:
                self.pf_load(h2, "h2T", _tl, 0)
            if it + 1 < len(_tl):
                self.pf_load(h2, "h2T", _tl, it + 1)
            self.emit("sp", lambda e, c0=c0, w=w: e.dma_start(out=xs[:, :, 0:w], in_=dr["xT"][:, :, c0:c0 + w]),
                      reads=[self.tb("xT", c0)], newgen=[xsb], dma=True)
            for c in range(32):
                bk = nxt()
                ps, psb = self.ps[bk], self.psb[bk]
                self.mm_group(ps[:, 0:w], psb, [(Wu[:, kc, c * 128:(c + 1) * 128], ht[:, kc, 0:w], [self.WAb, htb]) for kc in range(KC)])
                r, rb = rts[c % 2]
                if c % 2 == 0:
                    self.emit("act", lambda e, r=r, ps=ps, w=w: e.activation(out=r[:, 0:w], in_=ps[:, 0:w], func=AF.Relu),
                              reads=[psb], newgen=[rb])
                else:
                    self.emit("dve", lambda e, r=r, ps=ps, w=w: e.tensor_scalar(out=r[:, 0:w], in0=ps[:, 0:w], scalar1=0.0, scalar2=None,
                                                                               op0=ALU.max), reads=[psb], newgen=[rb])
                self.emit("pool", lambda e, r=r, c=c, w=w: e.tensor_tensor(out=u[:, c, 0:w], in0=r[:, 0:w], in1=r[:, 0:w], op=ALU.mult),
                          reads=[rb], **({"newgen": [ub]} if c == 0 else {"writes": [ub]}))
            for n2 in range(8):
                bk = nxt()
                ps, psb = self.ps[bk], self.psb[bk]
                self.mm_group(ps[:, 0:w], psb, [(Wd[:, c, n2 * 128:(n2 + 1) * 128], u[:, c, 0:w], [self.WBb, ub]) for c in range(32)])
                self.emit("dve", lambda e, ps=ps, n2=n2, w=w: e.tensor_tensor(out=xs[:, n2, 0:w], in0=ps[:, 0:w], in1=xs[:, n2, 0:w], op=ALU.add),
                          reads=[psb, xsb], writes=[xsb])
            if c0 == 0:
                self.zero_pad_cols(xs, xsb)
            if not last:
                hs, hsb = hn[it % 2]
                wc = (lambda kc: self.p_fn(kc)) if layer_mode else (lambda kc: self.p_n1(l + 1, kc))
                self.store_x_and_norm(xs, xsb, hs, hsb, c0, w, wc, "hT", sq_t, rstd, 7)
            elif c0 >= 128:
                self.rmsnorm(xs, xsb, w, lambda kc: self.p_fn(kc), ys, ysb, sq_t, rstd, 7)
                for j in range(w // 128):
                    yt, ytb = ytm[yi % 2]
                    yi += 1
                    for half in range(2):
                        bk = nxt()
                        ps, psb = self.ps[bk], self.psb[bk]
                        for q in range(4):
                            kc = half * 4 + q
                            self.emit("pe", lambda e, ps=ps, q=q, kc=kc, j=j: e.transpose(
                                out=ps[:, q * 128:(q + 1) * 128], in_=ys[:, kc, j * 128:(j + 1) * 128], identity=ident),
                                reads=[ysb, self.c32b], **({"newgen": [psb]} if q == 0 else {"writes": [psb]}))
                        dst = yt[:, half * 512:(half + 1) * 512]
                        if half == 0:
                            fn = lambda e, dst=dst, ps=ps: e.activation(out=dst, in_=ps[:, 0:512], func=AF.Copy)
                            eng = "act"
                        else:
                            fn = lambda e, dst=dst, ps=ps: e.tensor_copy(out=dst, in_=ps[:, 0:512])
                            eng = "dve"
                        self.emit(eng, fn, reads=[psb], **({"newgen": [ytb]} if half == 0 else {"writes": [ytb]}))
                    r0 = c0 - 128 + j * 128
                    self.emit("sp", lambda e, yt=yt, r0=r0: e.dma_start(out=dr["out"][r0:r0 + 128, :], in_=yt),
                              reads=[ytb], writes=[self.tb("out", c0)], dma=True)

    def phase_copy_in(self, with_h):
        self.barrier()
        self.phase_reset()
        dr = self.dr
        TW = 256
        xs_t = [self.t32(f"cx{i}", 8 * TW, a=8) for i in range(2)]
        hs_t = [self.t16(f"ch{i}", 8 * TW, a=8) for i in range(2)]
        sq_t = [self.t32(f"sq{i}", TW) for i in range(2)]
        rstd = self.t32("rstd", TW)
        for it, (c0, w) in enumerate(self.subtiles(TW)):
            xs, xsb = xs_t[it % 2]
            self.emit("sp", lambda e, xs=xs, c0=c0, w=w: e.dma_start(out=xs[:, :, 0:w], in_=dr["xT_in"][:, :, c0:c0 + w]),
                      newgen=[xsb], dma=True)
            if with_h:
                hs, hsb = hs_t[it % 2]
                self.store_x_and_norm(xs, xsb, hs, hsb, c0, w, lambda kc: self.p_n1(0, kc), "hT", sq_t, rstd, 7)
            else:
                self.emit("sp", lambda e, xs=xs, c0=c0, w=w: e.dma_start(out=dr["xT"][:, :, c0:c0 + w], in_=xs[:, :, 0:w]),
                          reads=[xsb], writes=[self.tb("xT", c0)], dma=True)

    def phase_post(self):
        self.barrier()
        self.phase_reset()
        dr = self.dr
        TW = 256
        xs_t = [self.t32(f"px{i}", 8 * TW, a=8) for i in range(2)]
        ys, ysb = self.t32("ys", 8 * TW, a=8)
        ytm = [self.t32(f"ytm{i}", 1024) for i in range(2)]
        sq_t = [self.t32(f"sq{i}", TW) for i in range(2)]
        rstd = self.t32("rstd", TW)
        nxt = self.psum_alloc([0, 1, 2, 3, 4, 5, 6])
        yi = 0
        for it, (c0, w) in enumerate(self.subtiles(TW)):
            if c0 < 128:
                continue
            xs, xsb = xs_t[it % 2]
            self.emit("sp", lambda e, xs=xs, c0=c0, w=w: e.dma_start(out=xs[:, :, 0:w], in_=dr["xT"][:, :, c0:c0 + w]),
                      reads=[self.tb("xT", c0)], newgen=[xsb], dma=True)
            yi = self.final_out(xs, xsb, ys, ysb, ytm, sq_t, rstd, nxt, c0, w, yi)

    def final_out(self, xs, xsb, ys, ysb, ytm, sq_t, rstd, nxt, c0, w, yi):
        dr = self.dr
        ident = self.cc("ident")
        self.rmsnorm(xs, xsb, w, lambda kc: self.p_fn(kc), ys, ysb, sq_t, rstd, 7)
        for j in range(w // 128):
            yt, ytb = ytm[yi % 2]
            yi += 1
            for half in range(2):
                bk = nxt()
                ps, psb = self.ps[bk], self.psb[bk]
                for q in range(4):
                    kc = half * 4 + q
                    self.emit("pe", lambda e, ps=ps, q=q, kc=kc, j=j: e.transpose(
                        out=ps[:, q * 128:(q + 1) * 128], in_=ys[:, kc, j * 128:(j + 1) * 128], identity=ident),
                        reads=[ysb, self.c32b], **({"newgen": [psb]} if q == 0 else {"writes": [psb]}))
                dst = yt[:, half * 512:(half + 1) * 512]
                if half == 0:
                    fn = lambda e, dst=dst, ps=ps: e.activation(out=dst, in_=ps[:, 0:512], func=AF.Copy)
                    eng = "act"
                else:
                    fn = lambda e, dst=dst, ps=ps: e.tensor_copy(out=dst, in_=ps[:, 0:512])
                    eng = "dve"
                self.emit(eng, fn, reads=[psb], **({"newgen": [ytb]} if half == 0 else {"writes": [ytb]}))
            r0 = c0 - 128 + j * 128
            self.emit("sp", lambda e, yt=yt, r0=r0: e.dma_start(out=dr["out"][r0:r0 + 128, :], in_=yt),
                      reads=[ytb], writes=[self.tb("out", c0)], dma=True)
        return yi

    def finish(self):
        self.barrier()


class Builder(Phases):
    def __init__(self, NT, DEPTH, debug=False, mode="fused"):
        self.mode = mode
        self.NT = NT
        self.DEPTH = DEPTH
        self.PD = 1 if mode == "loop" else DEPTH
        self.P = 128 + 512 * NT
        self.NB = 1 + 4 * NT
        self.debug = debug
        self.consts, self.ccols, self.aug, self.ND = make_consts(NT)

    def carve(self, words):
        off = self.aoff
        self.aoff += words
        assert self.aoff <= self.AW, (self.aoff, self.AW)
        return self.arena[:, off:off + words]

    def t32(self, name, n, a=None):
        v = self.carve(n)
        if a is not None:
            v = v.rearrange("p (a b) -> p a b", a=a)
        return (v, Buf(name))

    def t16(self, name, n, a=None):
        assert n % 2 == 0
        v = self.carve(n // 2).bitcast(BF16)
        if a is not None:
            v = v.rearrange("p (a b) -> p a b", a=a)
        return (v, Buf(name))

    def phase_reset(self):
        self.aoff = self.aoff_phase

    def tb(self, name, c0):
        t = 0 if c0 < 128 else 1 + (c0 - 128) // 512
        return self.db[name][t]

    def subtiles(self, tw):
        res = [(0, 128)]
        c = 128
        while c < self.P:
            res.append((c, tw))
            c += tw
        return res

    def build(self):
        NT, DEPTH, P = self.NT, self.DEPTH, self.P
        nc = bass.Bass("TRN2", target_bir_lowering=False)
        self.nc = nc
        S = P - 128
        dbg = "ExternalOutput" if self.debug else "Internal"
        mode = self.mode
        dr = {}
        if mode in ("fused", "pre", "loop"):
            dr["x"] = nc.dram_tensor("x", [S, D], F32, kind="ExternalInput").ap()
            dr["meta"] = nc.dram_tensor("meta", [NMETA, D], F32, kind="ExternalInput").ap()
        if mode in ("layer", "post"):
            dr["xT_in"] = nc.dram_tensor("xT_in", [128, KC, P], F32, kind="ExternalInput").ap()
        if mode in ("fused", "layer", "loop"):
            dr["w_in"] = nc.dram_tensor("w_in", [DEPTH, D, DIN], F32, kind="ExternalInput").ap()
            dr["w_branch"] = nc.dram_tensor("w_branch", [DEPTH, 3, D, D], F32, kind="ExternalInput").ap()
            dr["w_out"] = nc.dram_tensor("w_out", [DEPTH, D, D], F32, kind="ExternalInput").ap()
            dr["w_up"] = nc.dram_tensor("w_up", [DEPTH, D, DFF], F32, kind="ExternalInput").ap()
            dr["w_down"] = nc.dram_tensor("w_down", [DEPTH, DFF, D], F32, kind="ExternalInput").ap()
        PD = self.PD
        self.NPAR = PD * 8 * 2 + 8 + PD * 24 + PD + PD * 256 + 2 * PD
        if mode == "loop":
            dr["w_in_cur"] = nc.dram_tensor("w_in_cur", [1, D, DIN], F32, kind="Internal").ap()
            dr["w_branch_cur"] = nc.dram_tensor("w_branch_cur", [1, 3, D, D], F32, kind="Internal").ap()
            dr["w_out_cur"] = nc.dram_tensor("w_out_cur", [1, D, D], F32, kind="Internal").ap()
            dr["w_up_cur"] = nc.dram_tensor("w_up_cur", [1, D, DFF], F32, kind="Internal").ap()
            dr["w_down_cur"] = nc.dram_tensor("w_down_cur", [1, DFF, D], F32, kind="Internal").ap()
            self.wstb = {k: Buf("wst_" + k) for k in ("w_in", "w_branch", "w_out", "w_up", "w_down")}
            dr["params"] = nc.dram_tensor("params", [DEPTH, 128, self.NPAR], F32, kind="ExternalInput").ap()
        else:
            dr["params"] = nc.dram_tensor("params", [128, self.NPAR], F32, kind="ExternalInput").ap()
        dr["consts"] = nc.dram_tensor("consts", list(self.consts.shape), F32, kind="ExternalInput").ap()
        dr["aug"] = nc.dram_tensor("aug", [32, P], BF16, kind="ExternalInput").ap()
        if mode in ("fused", "post", "loop"):
            dr["out"] = nc.dram_tensor("out", [S, D], F32, kind="ExternalOutput").ap()
        xk = "ExternalOutput" if mode in ("pre", "layer") else dbg
        dr["xT"] = nc.dram_tensor("xT", [128, KC, P], F32, kind=xk).ap()
        dr["hT"] = nc.dram_tensor("hT", [128, KC, P], BF16, kind=dbg).ap()
        dr["h2T"] = nc.dram_tensor("h2T", [128, KC, P], BF16, kind=dbg).ap()
        dr["retT"] = nc.dram_tensor("retT", [128, KC, P], BF16, kind=dbg).ap()
        dr["convT"] = nc.dram_tensor("convT", [128, KC, P], BF16, kind=dbg).ap()
        dr["diffT"] = nc.dram_tensor("diffT", [128, KC, P], BF16, kind=dbg).ap()
        dr["dq"] = nc.dram_tensor("dq", [DH * 2, 65, P], BF16, kind=dbg).ap()
        dr["dk"] = nc.dram_tensor("dk", [DH * 2, 65, P], BF16, kind=dbg).ap()
        dr["dv"] = nc.dram_tensor("dv", [P, D], BF16, kind=dbg).ap()
        self.dr = dr
        self.db = {k: [Buf(f"{k}{t}") for t in range(NT + 1)] for k in
                   ("xT", "hT", "h2T", "retT", "convT", "diffT", "dq", "dk", "dv", "out")}
        self.db_aug = Buf("augrows")

        self.pg = Prog()
        with ExitStack() as es:
            self.es = es
            self.alloc_global()
            plan = [("seg", "main")]
            if mode == "loop":
                self.pg.start_seg("pre")
                self.phase_init()
                self.phase0()
                self.finish()
                import os
                if int(os.environ.get("K_LT", "9")) == -3:
                    self.pg.start_seg("pre2")
                    self.finish()
                self.pg.start_seg("body")
                lt = int(os.environ.get("K_LT", "9"))
                self.load_params(True)
                if lt >= 1:
                    self.stage_weights()
                self.lam_compute(0)
                for i, ph in enumerate((self.phase_p1a, self.phase_p1b, self.phase_p1c, self.phase_att, self.phase_p2, self.phase_mlp)):
                    if lt >= 2 + i:
                        ph(0)
                self.finish()
                self.pg.start_seg("post")
                self.phase_post()
                self.finish()
                plan = [("seg", "pre"), ("loop", "body", DEPTH), ("seg", "post")]
                if lt == -1:
                    plan = [("seg", "pre"), ("seg", "post")]
                if lt == -2:
                    plan = [("seg", "pre")]
                if lt == -3:
                    plan = [("seg", "pre"), ("seg", "pre2")]
            else:
                self.phase_init()
                if mode in ("fused", "pre"):
                    self.phase0()
                if mode == "layer":
                    self.phase_copy_in(True)
                if mode == "post":
                    self.phase_copy_in(False)
                if mode in ("fused", "layer"):
                    for l in range(DEPTH):
                        for ph in (self.phase_p1a, self.phase_p1b, self.phase_p1c, self.phase_att, self.phase_p2, self.phase_mlp):
                            ph(l)
                if mode == "post":
                    self.phase_post()
                self.finish()
            csem = {e: es.enter_context(nc.semaphore(f"c_{e}")) for e in Prog.ENGS}
            dsem = {e: [es.enter_context(nc.semaphore(f"d_{e}{i}")) for i in range(n)]
                    for e, n in Prog.NDMASEM.items()}
            barA = es.enter_context(nc.semaphore("barA"))
            barB = es.enter_context(nc.semaphore("barB"))
            with nc.Block() as block:
                self.pg.replay(nc, block, csem, dsem, plan, barA, barB)
        return nc


def make_params(DEPTH, norm1_w, conv_w, diff_lambda, diff_subln_w, norm2_w, final_norm_w, layer0=0):
    cols = []
    cols.append(norm1_w[:DEPTH].reshape(DEPTH, 8, 128).transpose(2, 0, 1).reshape(128, DEPTH * 8))
    cols.append(norm2_w[:DEPTH].reshape(DEPTH, 8, 128).transpose(2, 0, 1).reshape(128, DEPTH * 8))
    cols.append(final_norm_w.reshape(8, 128).T)
    cols.append(conv_w[:DEPTH].reshape(DEPTH, 3, 8, 128).transpose(3, 0, 1, 2).reshape(128, DEPTH * 24))
    cols.append(diff_subln_w[:DEPTH].T)
    cols.append(np.broadcast_to(diff_lambda[:DEPTH].reshape(1, DEPTH * 256), (128, DEPTH * 256)))
    lamc = np.zeros((128, 2 * DEPTH), np.float32)
    for l in range(DEPTH):
        li = layer_lam_init(layer0 + l)
        lamc[:, 2 * l] = -li
        lamc[:, 2 * l + 1] = math.log(1.0 - li)
    cols.append(lamc)
    return np.ascontiguousarray(np.concatenate(cols, axis=1), dtype=np.float32)


_CACHE = {}


def run(inputs, NT, DEPTH, ncores, debug=False, trace=False):
    key = (NT, DEPTH, debug)
    if key not in _CACHE:
        b = Builder(NT, DEPTH, debug)
        nc = b.build()
        _CACHE[key] = (b, nc)
    b, nc = _CACHE[key]
    f = lambda a: np.ascontiguousarray(np.asarray(a, dtype=np.float32))
    S = 512 * NT
    params = make_params(DEPTH, f(inputs["norm1_w"]), f(inputs["conv_w"]), f(inputs["diff_lambda"]),
                         f(inputs["diff_subln_w"]), f(inputs["norm2_w"]), f(inputs["final_norm_w"]))
    shared = {
        "meta": f(inputs["meta_tokens"]),
        "w_in": f(inputs["w_in"])[:DEPTH], "w_branch": f(inputs["w_branch"])[:DEPTH],
        "w_out": f(inputs["w_out"])[:DEPTH], "w_up": f(inputs["w_up"])[:DEPTH], "w_down": f(inputs["w_down"])[:DEPTH],
        "params": params, "consts": b.consts, "aug": b.aug,
    }
    x = f(inputs["x"])
    in_maps = [dict(shared, x=np.ascontiguousarray(x[c, :S])) for c in range(ncores)]
    res = run_bass_kernel_spmd(nc, in_maps, core_ids=list(range(ncores)), **({"trace": True} if trace else {}))
    out = np.stack([res.results[c]["out"] for c in range(ncores)], axis=0)
    return out, res


def _prog(NT, mode):
    key = (NT, mode)
    if key not in _CACHE:
        b = Builder(NT, 1, False, mode=mode)
        _CACHE[key] = (b, b.build())
    return _CACHE[key]


def run_unfused(inputs, NT, DEPTH, ncores):
    f = lambda a: np.ascontiguousarray(np.asarray(a, dtype=np.float32))
    S = 512 * NT
    x = f(inputs["x"])
    n1, n2, fnw = f(inputs["norm1_w"]), f(inputs["norm2_w"]), f(inputs["final_norm_w"])
    cw, dl, sl = f(inputs["conv_w"]), f(inputs["diff_lambda"]), f(inputs["diff_subln_w"])
    cores = list(range(ncores))

    def params(l, nxt):
        return make_params(1, n1[l:l + 1], cw[l:l + 1], dl[l:l + 1], sl[l:l + 1], n2[l:l + 1], nxt, layer0=l)

    b, nc = _prog(NT, "pre")
    shared = {"meta": f(inputs["meta_tokens"]), "params": params(0, fnw), "consts": b.consts, "aug": b.aug}
    res = run_bass_kernel_spmd(nc, [dict(shared, x=np.ascontiguousarray(x[c, :S])) for c in cores], core_ids=cores)
    xT = [res.results[c]["xT"] for c in cores]
    b, nc = _prog(NT, "layer")
    for l in range(DEPTH):
        nxt = n1[l + 1] if l + 1 < DEPTH else fnw
        shared = {"w_in": f(inputs["w_in"][l:l + 1]), "w_branch": f(inputs["w_branch"][l:l + 1]),
                  "w_out": f(inputs["w_out"][l:l + 1]), "w_up": f(inputs["w_up"][l:l + 1]),
                  "w_down": f(inputs["w_down"][l:l + 1]), "params": params(l, nxt), "consts": b.consts, "aug": b.aug}
        res = run_bass_kernel_spmd(nc, [dict(shared, xT_in=xT[c]) for c in cores], core_ids=cores)
        xT = [res.results[c]["xT"] for c in cores]
    b, nc = _prog(NT, "post")
    shared = {"params": params(0, fnw), "consts": b.consts, "aug": b.aug}
    res = run_bass_kernel_spmd(nc, [dict(shared, xT_in=xT[c]) for c in cores], core_ids=cores)
    return np.stack([res.results[c]["out"] for c in cores], axis=0)


def run_loop(inputs, NT, DEPTH, ncores, trace=False):
    key = (NT, DEPTH, "loop")
    if key not in _CACHE:
        b = Builder(NT, DEPTH, False, mode="loop")
        _CACHE[key] = (b, b.build())
    b, nc = _CACHE[key]
    f = lambda a: np.ascontiguousarray(np.asarray(a, dtype=np.float32))
    S = 512 * NT
    x = f(inputs["x"])
    n1, n2, fnw = f(inputs["norm1_w"]), f(inputs["norm2_w"]), f(inputs["final_norm_w"])
    cw, dl, sl = f(inputs["conv_w"]), f(inputs["diff_lambda"]), f(inputs["diff_subln_w"])
    pars = []
    for l in range(DEPTH):
        nxt = n1[l + 1] if l + 1 < DEPTH else fnw
        pars.append(make_params(1, n1[l:l + 1], cw[l:l + 1], dl[l:l + 1], sl[l:l + 1], n2[l:l + 1], nxt, layer0=l))
    shared = {
        "meta": f(inputs["meta_tokens"]),
        "w_in": f(inputs["w_in"])[:DEPTH], "w_branch": f(inputs["w_branch"])[:DEPTH],
        "w_out": f(inputs["w_out"])[:DEPTH], "w_up": f(inputs["w_up"])[:DEPTH], "w_down": f(inputs["w_down"])[:DEPTH],
        "params": np.ascontiguousarray(np.stack(pars, axis=0)), "consts": b.consts, "aug": b.aug,
    }
    cores = list(range(ncores))
    in_maps = [dict(shared, x=np.ascontiguousarray(x[c, :S])) for c in cores]
    res = run_bass_kernel_spmd(nc, in_maps, core_ids=cores, **({"trace": True} if trace else {}))
    return np.stack([res.results[c]["out"] for c in cores], axis=0), res


def kernel(x, meta_tokens, norm1_w, w_in, conv_w, diff_lambda, diff_subln_w, w_branch, w_out,
           norm2_w, w_up, w_down, final_norm_w):
    inputs = dict(x=x, meta_tokens=meta_tokens, norm1_w=norm1_w, w_in=w_in, conv_w=conv_w, diff_lambda=diff_lambda,
                  diff_subln_w=diff_subln_w, w_branch=w_branch, w_out=w_out, norm2_w=norm2_w, w_up=w_up,
                  w_down=w_down, final_norm_w=final_norm_w)
    out, _ = run_loop(inputs, NT=16, DEPTH=4, ncores=8)
    return out.astype(np.float32)
```
